# Optimizing a Trainium2 kernel written in Bass

```python
import math
import jax, jax.numpy as jnp
from jax import lax
import numpy as np

D_MODEL = 1024
BATCH = 4
SEQ = 4096
DEPTH = 2

N_A_LAYERS = DEPTH // 2
N_B_LAYERS = DEPTH - N_A_LAYERS

RET_HEADS = 4
RET_QK_DIM = 256
RET_V_DIM = 512
RET_WIDTH = RET_HEADS * RET_V_DIM
RET_QK_WIDTH = RET_HEADS * RET_QK_DIM
RET_CHUNK = 128
ROPE_BASE = 10000.0

SB_HEADS = 8
SB_QK_DIM = 128
SB_V_DIM = 256
SB_QK_WIDTH = SB_HEADS * SB_QK_DIM
SB_WIDTH = SB_HEADS * SB_V_DIM
SB_BLOCK = 128

EPS = 1e-6

kernel_name = "yoco_retention_stickbreaking_hybrid"


def rms_norm(x, g):
    xf = x.astype(jnp.float32)
    y = xf * lax.rsqrt(jnp.mean(xf * xf, axis=-1, keepdims=True) + EPS)
    return (y * g.astype(jnp.float32)).astype(x.dtype)


def split_heads(t, n_heads):
    b, s, w = t.shape
    return t.reshape(b, s, n_heads, w // n_heads).transpose(0, 2, 1, 3)


def merge_heads(t):
    b, h, s, d = t.shape
    return t.transpose(0, 2, 1, 3).reshape(b, s, h * d)


def rope(x, pos):
    half = x.shape[-1] // 2
    inv_freq = 1.0 / (ROPE_BASE ** (jnp.arange(half, dtype=jnp.float32) / half))
    ang = pos[:, None] * inv_freq[None, :]
    cos, sin = jnp.cos(ang), jnp.sin(ang)
    x1, x2 = x[..., :half], x[..., half:]
    return jnp.concatenate([x1 * cos - x2 * sin, x1 * sin + x2 * cos], axis=-1)


def retention_chunkwise(q, k, v):
    b, h, t, dk = q.shape
    dv = v.shape[-1]
    c = RET_CHUNK
    nc = t // c
    log_gamma = jnp.log1p(-jnp.exp2(-5.0 - jnp.arange(h, dtype=jnp.float32)))
    idx = jnp.arange(c, dtype=jnp.float32)
    diff = idx[:, None] - idx[None, :]
    intra_decay = jnp.where(diff >= 0, jnp.exp(log_gamma[:, None, None] * diff), 0.0)
    q_decay = jnp.exp(log_gamma[:, None] * (idx[None, :] + 1.0))
    k_decay = jnp.exp(log_gamma[:, None] * (c - 1.0 - idx[None, :]))
    chunk_decay = jnp.exp(log_gamma * c)

    def to_chunks(a):
        return a.reshape(b, h, nc, c, a.shape[-1]).transpose(2, 0, 1, 3, 4)

    def step(state, inp):
        qc, kc, vc = inp
        scores = jnp.einsum('bhid,bhjd->bhij', qc, kc) * intra_decay
        o = jnp.einsum('bhij,bhje->bhie', scores, vc)
        o = o + jnp.einsum('bhid,bhde->bhie', qc * q_decay[None, :, :, None], state)
        state = state * chunk_decay[None, :, None, None] + jnp.einsum(
            'bhjd,bhje->bhde', kc * k_decay[None, :, :, None], vc)
        return state, o

    state0 = jnp.zeros((b, h, dk, dv), jnp.float32)
    _, out = lax.scan(step, state0, (to_chunks(q), to_chunks(k), to_chunks(v)))
    return out.transpose(1, 2, 0, 3, 4).reshape(b, h, t, dv)


def stick_breaking_attention(q, k, v):
    b, h, t, dk = q.shape
    nb = t // SB_BLOCK
    scale = 1.0 / math.sqrt(dk)
    key_idx = jnp.arange(t)
    q_blocks = q.reshape(b, h, nb, SB_BLOCK, dk).transpose(2, 0, 1, 3, 4)

    def block(args):
        qb, blk = args
        t_idx = blk * SB_BLOCK + jnp.arange(SB_BLOCK)
        z = jnp.einsum('bhqd,bhkd->bhqk', qb, k) * scale
        causal = key_idx[None, :] < t_idx[:, None]
        log_not = jnp.where(causal, jax.nn.log_sigmoid(-z), 0.0)
        suffix = lax.cumsum(log_not, axis=3, reverse=True) - log_not
        weights = jnp.where(causal, jnp.exp(jax.nn.log_sigmoid(z) + suffix), 0.0)
        return jnp.einsum('bhqk,bhke->bhqe', weights, v)

    out = lax.map(block, (q_blocks, jnp.arange(nb)))
    return out.transpose(1, 2, 0, 3, 4).reshape(b, h, t, v.shape[-1])


def setup_inputs(seed: int = 0) -> dict:
    key = jax.random.key(seed)
    ks = jax.random.split(key, 12)
    d = D_MODEL
    ret_in_cols = 2 * RET_QK_WIDTH + 2 * RET_WIDTH
    sb_in_cols = SB_QK_WIDTH + SB_WIDTH
    kv_cols = SB_QK_WIDTH + SB_WIDTH

    def gain(k, shape):
        return 1.0 + 0.02 * jax.random.normal(k, shape, jnp.float32)

    def dense(k, shape, fan_in):
        return jax.random.normal(k, shape, jnp.float32) * (fan_in ** -0.5)

    return {
        "x": jax.random.normal(ks[0], (BATCH, SEQ, d), jnp.float32),
        "ret_norm_pre": gain(ks[1], (N_A_LAYERS, d)),
        "ret_w_in": dense(ks[2], (N_A_LAYERS, d, ret_in_cols), d),
        "ret_w_out": dense(ks[3], (N_A_LAYERS, RET_WIDTH, d), RET_WIDTH),
        "ret_norm_post": gain(ks[4], (N_A_LAYERS, d)),
        "kv_norm": gain(ks[5], (d,)),
        "w_kv": dense(ks[6], (d, kv_cols), d),
        "sb_norm_pre": gain(ks[7], (N_B_LAYERS, d)),
        "sb_w_in": dense(ks[8], (N_B_LAYERS, d, sb_in_cols), d),
        "sb_w_out": dense(ks[9], (N_B_LAYERS, SB_WIDTH, d), SB_WIDTH),
        "sb_norm_post": gain(ks[10], (N_B_LAYERS, d)),
    }


def reference(x, ret_norm_pre, ret_w_in, ret_w_out, ret_norm_post, kv_norm, w_kv,
              sb_norm_pre, sb_w_in, sb_w_out, sb_norm_post):
    b, t, _ = x.shape
    pos = jnp.arange(t, dtype=jnp.float32)
    h = x
    k_shared = None
    v_shared = None
    for i in range(DEPTH):
        if i < N_A_LAYERS:
            u = rms_norm(h, ret_norm_pre[i])
            proj = u @ ret_w_in[i]
            q, k, v, g = jnp.split(
                proj, [RET_QK_WIDTH, 2 * RET_QK_WIDTH, 2 * RET_QK_WIDTH + RET_WIDTH], axis=-1)
            q = rope(split_heads(q, RET_HEADS).astype(jnp.float32), pos)
            k = rope(split_heads(k, RET_HEADS).astype(jnp.float32), pos) * (RET_QK_DIM ** -0.5)
            v = split_heads(v, RET_HEADS).astype(jnp.float32)
            o = retention_chunkwise(q, k, v)
            mu = jnp.mean(o, axis=-1, keepdims=True)
            var = jnp.mean(jnp.square(o - mu), axis=-1, keepdims=True)
            o = (o - mu) * lax.rsqrt(var + EPS)
            o = merge_heads(o).astype(h.dtype) * jax.nn.silu(g)
            y = o @ ret_w_out[i]
            h = h + rms_norm(y, ret_norm_post[i])
            if i == N_A_LAYERS - 1:
                kv = rms_norm(h, kv_norm) @ w_kv
                k_sh, v_sh = jnp.split(kv, [SB_QK_WIDTH], axis=-1)
                k_shared = split_heads(k_sh, SB_HEADS).astype(jnp.float32)
                v_shared = split_heads(v_sh, SB_HEADS).astype(jnp.float32)
        else:
            j = i - N_A_LAYERS
            u = rms_norm(h, sb_norm_pre[j])
            proj = u @ sb_w_in[j]
            q, g = jnp.split(proj, [SB_QK_WIDTH], axis=-1)
            q = split_heads(q, SB_HEADS).astype(jnp.float32)
            o = stick_breaking_attention(q, k_shared, v_shared)
            o = merge_heads(o).astype(h.dtype) * jax.nn.silu(g)
            y = o @ sb_w_out[j]
            h = h + rms_norm(y, sb_norm_post[j])
    return h
```

```python
import math
from contextlib import ExitStack

import numpy as np
import concourse.bass as bass
import concourse.mybir as mybir
from concourse.bass_utils import run_bass_kernel_spmd

F32 = mybir.dt.float32
BF16 = mybir.dt.bfloat16
I32 = mybir.dt.int32
AF = mybir.ActivationFunctionType
ALU = mybir.AluOpType

T = 4096
D = 1024
NCH = 32
EPS = 1e-6
ENGS = ("pe", "act", "dve", "pool", "sp")
NDMA = 24


class Res:
    __slots__ = ("w", "rs")

    def __init__(self):
        self.w = None
        self.rs = []


class Op:
    __slots__ = ("eng", "idx", "fn", "waits", "flag", "dma", "val", "pre")

    def __init__(self, eng, idx, fn):
        self.eng = eng
        self.idx = idx
        self.fn = fn
        self.waits = []
        self.flag = False
        self.dma = None
        self.val = None
        self.pre = None


class Sched:
    def __init__(self, ndma=NDMA):
        self.q = {e: [] for e in ENGS}
        self.seen = {e: {} for e in ENGS}
        self.ndma = ndma
        self.dma_n = {e: 0 for e in ENGS}
        self.dma_last = {}
        self.last = {e: None for e in ENGS}

    def _add_wait(self, op, prod):
        if prod is None or prod is op:
            return
        if prod.dma is None:
            if prod.eng == op.eng and op.eng == "pe" and op.dma is None:
                return
            key = prod.eng
        else:
            key = ("d", prod.eng, prod.dma[0])
        s = self.seen[op.eng]
        if s.get(key, -1) >= prod.idx:
            return
        s[key] = prod.idx
        prod.flag = True
        op.waits.append(prod)

    def op(self, eng, fn, reads=(), writes=(), dma=False, after=()):
        q = self.q[eng]
        o = Op(eng, len(q), fn)
        if dma:
            n = self.dma_n[eng]
            slot = n % self.ndma
            o.dma = (slot, 16 * (n // self.ndma + 1))
            self.dma_n[eng] = n + 1
            o.pre = self.dma_last.get((eng, slot))
            self.dma_last[(eng, slot)] = o
            o.idx = n
        for p in after:
            self._add_wait(o, p)
        for r in reads:
            self._add_wait(o, r.w)
        for r in writes:
            self._add_wait(o, r.w)
            for t in r.rs:
                self._add_wait(o, t)
        q.append(o)
        for r in reads:
            r.rs.append(o)
        for r in writes:
            r.w = o
            r.rs = []
        if not dma:
            self.last[eng] = o
        return o

    def barrier(self):
        prods = [o for o in self.last.values() if o is not None] + list(self.dma_last.values())
        for e in ENGS:
            self.op(e, lambda g: g.nop(), after=prods)

    def emit(self, block, esem, dsem):
        for e in ENGS:
            c = 0
            for o in self.q[e]:
                if o.dma is None and o.flag:
                    c += 1
                    o.val = c

        def tok(p):
            if p.dma is None:
                return esem[p.eng], p.val
            return dsem[p.eng][p.dma[0]], p.dma[1]

        def run(e, g):
            for o in self.q[e]:
                if o.pre is not None:
                    s, v = tok(o.pre)
                    g.wait_ge(s, v)
                for p in o.waits:
                    s, v = tok(p)
                    g.wait_ge(s, v)
                ins = o.fn(g)
                if o.dma is not None:
                    ins.then_inc(dsem[e][o.dma[0]], 16)
                elif o.flag:
                    ins.then_inc(esem[e], 1)
            for (qe, _slot), o in self.dma_last.items():
                if qe == e:
                    s, v = tok(o)
                    g.wait_ge(s, v)

        @block.tensor
        def _(g):
            run("pe", g)

        @block.scalar
        def _(g):
            run("act", g)

        @block.vector
        def _(g):
            run("dve", g)

        @block.gpsimd
        def _(g):
            run("pool", g)

        @block.sync
        def _(g):
            run("sp", g)


def _host_consts():
    half = 128
    inv_freq = (1.0 / (np.float32(10000.0) ** (np.arange(half, dtype=np.float32) / np.float32(half)))).astype(np.float32)
    pos = np.arange(T, dtype=np.float32)
    ang = (pos[:, None] * inv_freq[None, :]).astype(np.float32)
    cossin = np.stack([np.cos(ang), np.sin(ang), np.sin(ang), np.cos(ang)], axis=1).astype(np.float32)
    lg = np.log1p(-np.exp2(-5.0 - np.arange(4, dtype=np.float64)))
    idx = np.arange(128, dtype=np.float64)
    dtm = np.zeros((128, 4, 128), np.float64)
    for h in range(4):
        dj = np.exp(-lg[h] * (idx + 1.0))
        dtm[:, h, :] = np.where(idx[:, None] <= idx[None, :], dj[:, None], 0.0)
    qdec = np.broadcast_to(np.exp(lg[None, :, None] * (idx[None, None, :] + 1.0)), (128, 4, 128))
    kdec = np.exp(lg[None, :] * (127.0 - idx[:, None]))
    cdec = [float(np.exp(lg[h] * 128.0)) for h in range(4)]
    return (cossin, dtm.astype(np.float32), np.ascontiguousarray(qdec).astype(np.float32),
            kdec.astype(np.float32), cdec)


def _masks(r):
    i = np.arange(128)[:, None]
    m = np.arange(128)[None, :]
    diag = (m <= 127 - i).astype(np.float32)
    full = np.ones((128, 128), np.float32)
    none = np.zeros((128, 128), np.float32)
    if r == 0:
        a = np.concatenate([full, diag], 1)
        b = np.concatenate([diag, none], 1)
    else:
        a = np.concatenate([diag, none], 1)
        b = np.concatenate([full, diag], 1)
    return np.stack([a, b], axis=1).astype(np.float32)


_CDEC = _host_consts()[4]


def build_nc(dbg=False):
    nc = bass.Bass("TRN2", target_bir_lowering=False)
    okind = "ExternalOutput" if dbg else "Internal"

    def din(name, shape, dt=F32):
        return nc.dram_tensor(name, shape, dt, kind="ExternalInput").ap()

    x = din("x", [T, D])
    w_in0 = din("w_in0", [D, 6144])
    w_out0 = din("w_out0", [2048, D])
    w_kv = din("w_kv", [D, 3072])
    w_in1 = din("w_in1", [D, 3072])
    w_out1 = din("w_out1", [2048, D])
    gpre = din("gpre", [128, 3, 8])
    gpost = din("gpost", [2, D])
    cossin = din("cossin", [T, 4, 128])
    dtm_d = din("dtm", [128, 4, 128])
    qdec_d = din("qdec", [128, 4, 128])
    kdec_d = din("kdec", [128, 4])
    masks_d = din("masks", [128, 2, 256])
    rid = din("rid", [1, 2], I32)
    out = nc.dram_tensor("out", [2048, D], F32, kind="ExternalOutput").ap()
    og_scr = nc.dram_tensor("og_scr", [T, 2048], BF16, kind=okind).ap()
    h1_scr = nc.dram_tensor("h1_scr", [T, D], F32, kind=okind).ap()
    kT_scr = nc.dram_tensor("kT_scr", [8, 128, T], BF16, kind=okind).ap()
    v_scr = nc.dram_tensor("v_scr", [T, 2048], BF16, kind=okind).ap()

    S = Sched()
    R = Res
    r_og_scr = [R() for _ in range(NCH)]
    r_h1_scr = [R() for _ in range(NCH)]
    r_kT_scr = [R() for _ in range(8)]
    r_v_scr = R()
    r_out = R()

    with ExitStack() as top:
        esem = {e: top.enter_context(nc.semaphore("es_" + e)) for e in ENGS}
        dsem = {"sp": [top.enter_context(nc.semaphore(f"ds_{i}")) for i in range(NDMA)]}
        reg0 = top.enter_context(nc.sync.register("reg0"))
        reg1 = top.enter_context(nc.sync.register("reg1"))
        regs = {}

        def sbt(st, name, shape, dt=F32):
            return st.enter_context(nc.sbuf_tensor(name, shape, dt))

        def pst(st, name, shape, dt=F32):
            return st.enter_context(nc.psum_tensor(name, shape, dt))

        ident = sbt(top, "ident", [128, 128], BF16); r_ident = R()
        jrev = sbt(top, "jrev", [128, 128], BF16); r_jrev = R()
        gpre_s = sbt(top, "gpre_s", [128, 4, 8]); r_gpre = R()
        ss = sbt(top, "ss", [128, 8]); r_ss = R()
        nhalf = sbt(top, "nhalf", [128, 1]); r_nhalf = R()
        zeros = sbt(top, "zeros", [128, 512]); r_zeros = R()

        def dma(out_ap, in_ap, reads, writes):
            return S.op("sp", lambda g: g.dma_start(out=out_ap, in_=in_ap), reads=reads, writes=writes, dma=True)

        def mm(out_ap, lhsT, rhs, start, stop, reads, writes):
            return S.op("pe", lambda g: g.matmul(out_ap, lhsT=lhsT, rhs=rhs, start=start, stop=stop),
                        reads=reads, writes=writes)

        def tr(out_ap, in_ap, reads, writes):
            return S.op("pe", lambda g: g.transpose(out=out_ap, in_=in_ap, identity=ident[:]),
                        reads=list(reads) + [r_ident], writes=writes)

        def act(out_ap, in_ap, func, reads, writes, scale=None, accum=None):
            kw = {}
            if scale is not None:
                kw["scale"] = scale
            if accum is not None:
                kw["accum_out"] = accum
            return S.op("act", lambda g: g.activation(out=out_ap, in_=in_ap, func=func, **kw),
                        reads=reads, writes=writes)

        def tt(eng, out_ap, in0, in1, op, reads, writes):
            return S.op(eng, lambda g: g.tensor_tensor(out=out_ap, in0=in0, in1=in1, op=op), reads=reads, writes=writes)

        def ts(eng, out_ap, in0, s1, s2, op0, op1, reads, writes):
            if s2 is None:
                return S.op(eng, lambda g: g.tensor_scalar(out=out_ap, in0=in0, scalar1=s1, scalar2=None, op0=op0),
                            reads=reads, writes=writes)
            return S.op(eng, lambda g: g.tensor_scalar(out=out_ap, in0=in0, scalar1=s1, scalar2=s2, op0=op0, op1=op1),
                        reads=reads, writes=writes)

        def stt(out_ap, in0, scalar, in1, op0, op1, reads, writes):
            return S.op("dve", lambda g: g.scalar_tensor_tensor(out=out_ap, in0=in0, scalar=scalar, in1=in1,
                                                                 op0=op0, op1=op1), reads=reads, writes=writes)

        def cp(eng, out_ap, in_ap, reads, writes):
            if eng == "act":
                return act(out_ap, in_ap, AF.Copy, reads, writes)
            return S.op(eng, lambda g: g.tensor_copy(out=out_ap, in_=in_ap), reads=reads, writes=writes)

        def rstd_from_ss(col, n, r_src):
            ts("dve", ss[:, col + 1:col + 2], ss[:, col:col + 1], 1.0 / n, EPS, ALU.mult, ALU.add, [r_src], [r_src])
            tt("pool", ss[:, col + 1:col + 2], ss[:, col + 1:col + 2], nhalf[:], ALU.pow, [r_src, r_nhalf], [r_src])

        with ExitStack() as c0:
            onesf = sbt(c0, "onesf", [128, 128]); r_onesf = R()
            tmpf = sbt(c0, "tmpf", [128, 128]); r_tmpf = R()
            S.op("pool", lambda g: g.memset(onesf[:], 1.0), writes=[r_onesf])
            S.op("pool", lambda g: g.memset(nhalf[:], -0.5), writes=[r_nhalf])
            S.op("pool", lambda g: g.memset(zeros[:], 0.0), writes=[r_zeros])
            S.op("pool", lambda g: g.affine_select(out=tmpf[:], in_=onesf[:], pattern=[[-1, 128]], compare_op=ALU.is_equal,
                                                   fill=0.0, base=0, channel_multiplier=1), reads=[r_onesf], writes=[r_tmpf])
            cp("dve", ident[:], tmpf[:], [r_tmpf], [r_ident])
            S.op("pool", lambda g: g.affine_select(out=tmpf[:], in_=onesf[:], pattern=[[1, 128]], compare_op=ALU.is_equal,
                                                   fill=0.0, base=-127, channel_multiplier=1), reads=[r_onesf], writes=[r_tmpf])
            cp("dve", jrev[:], tmpf[:], [r_tmpf], [r_jrev])
            dma(gpre_s[:, 0:3, :], gpre, [], [r_gpre])
            ts("dve", gpre_s[:, 3, :], gpre_s[:, 0, :], 1.0 / 16.0, None, ALU.mult, None, [r_gpre], [r_gpre])

            def load_regs(g):
                g.reg_load(reg0, rid[0:1, 0:1])
                g.reg_load(reg1, rid[0:1, 1:2])
                regs["r"] = g.snap(reg0, min_val=0, max_val=1)
                regs["nr"] = g.snap(reg1, min_val=0, max_val=1)
                regs["r128"] = regs["r"] * 128
                regs["n128"] = regs["nr"] * 128
                return g.nop()
            S.op("sp", load_regs)
            S.barrier()

        rr = [0]

        def load_w(st, name, w_dram, nk, ncols, stg, r_stg, gcol_of_piece, piece=1024):
            wt = sbt(st, name, [128, nk, ncols], BF16)
            r_w = R()
            for kc in range(nk):
                for c0_ in range(0, ncols, piece):
                    sl = rr[0] % 2
                    dma(stg[:, sl, 0:piece], w_dram[kc * 128:(kc + 1) * 128, c0_:c0_ + piece], [], [r_stg[sl]])
                    gi = gcol_of_piece(c0_)
                    eng = ("dve", "act")[rr[0] % 2]
                    o_ap = wt[:, kc, c0_:c0_ + piece]
                    i_ap = stg[:, sl, 0:piece]
                    if gi is None:
                        cp(eng, o_ap, i_ap, [r_stg[sl]], [r_w])
                    elif eng == "act":
                        act(o_ap, i_ap, AF.Identity, [r_stg[sl], r_gpre], [r_w], scale=gpre_s[:, gi, kc:kc + 1])
                    else:
                        ts(eng, o_ap, i_ap, gpre_s[:, gi, kc:kc + 1], None, ALU.mult, None, [r_stg[sl], r_gpre], [r_w])
                    rr[0] += 1
            return wt, r_w

        with ExitStack() as pa:
            stg = sbt(pa, "stgA", [128, 2, 1024]); r_stg = [R(), R()]
            w0, r_w0 = load_w(pa, "w0", w_in0, 8, 6144, stg, r_stg,
                              lambda c: 3 if 1024 <= c < 2048 else 0)
            dtm = sbt(pa, "dtm_s", [128, 4, 128]); r_dtm = R()
            qdec = sbt(pa, "qdec_s", [128, 4, 128]); r_qdec = R()
            kdec = sbt(pa, "kdec_s", [128, 4]); r_kdec = R()
            dma(dtm[:], dtm_d, [], [r_dtm])
            dma(qdec[:], qdec_d, [], [r_qdec])
            dma(kdec[:], kdec_d, [], [r_kdec])
            Sst = sbt(pa, "Sst", [128, 4, 2, 512]); r_S = [[R(), R()] for _ in range(4)]
            Sb = sbt(pa, "Sb", [128, 4, 2, 512], BF16); r_Sb = [[R(), R()] for _ in range(4)]
            xs = sbt(pa, "xsA", [128, 2, D]); r_xs = [R(), R()]
            cs = sbt(pa, "csA", [128, 2, 4, 128]); r_cs = [R(), R()]
            junk = sbt(pa, "junkA", [128, D], BF16); r_junk = R()
            u = sbt(pa, "uA", [128, D], BF16); r_u = R()
            uT = sbt(pa, "uTA", [128, 8, 128], BF16); r_uT = R()
            qk = sbt(pa, "qkA", [128, 2048], BF16); r_qk = [R() for _ in range(4)]
            rt = sbt(pa, "rtA", [128, 2, 2, 2, 128]); r_rt = [R() for _ in range(2)]
            qdT = sbt(pa, "qdTA", [128, 8, 128], BF16); r_qdT = R()
            kT = sbt(pa, "kTA", [128, 8, 128], BF16); r_kT = R()
            kd = sbt(pa, "kdA", [128, 1024], BF16); r_kd = R()
            v = sbt(pa, "vA", [128, 2, 2048], BF16); r_v = [[R() for _ in range(4)] for _ in range(2)]
            gate = sbt(pa, "gateA", [128, 2, 2048], BF16); r_gate = [[R() for _ in range(4)] for _ in range(2)]
            sg = sbt(pa, "sgA", [128, 2, 512]); r_sg = [R(), R()]
            stm = sbt(pa, "stmA", [128, 2, 128], BF16); r_stm = [R(), R()]
            og = sbt(pa, "ogA", [128, 2, 2048], BF16); r_og = [R(), R()]
            nt = sbt(pa, "ntA", [128, 2, 512]); r_nt = [R(), R()]
            bn6 = sbt(pa, "bn6A", [128, 2, 6]); bn2 = sbt(pa, "bn2A", [128, 2, 4]); r_bn = [R(), R()]
            pA = [pst(pa, f"pA{i}", [128, 512]) for i in range(2)]; r_pA = [R(), R()]
            pT = [pst(pa, f"pT{i}", [128, 1024], BF16) for i in range(2)]; r_pT = [R(), R()]
            pST = pst(pa, "pST", [128, 512]); r_pST = R()
            pO = pst(pa, "pO", [128, 512]); r_pO = R()
            pDS = [pst(pa, f"pDS{i}", [128, 512]) for i in range(2)]; r_pDS = [R(), R()]

            def loadA(c):
                sl = c % 2
                dma(xs[:, sl, :], x[c * 128:(c + 1) * 128, :], [], [r_xs[sl]])
                dma(cs[:, sl, :, :], cossin[c * 128:(c + 1) * 128, :, :], [], [r_cs[sl]])

            gcntA = [0]

            def xpre(c):
                sl = c % 2
                act(junk[:], xs[:, sl, :], AF.Square, [r_xs[sl]], [r_junk, r_ss], accum=ss[:, 0:1])
                rstd_from_ss(0, D, r_ss)
                act(u[:], xs[:, sl, :], AF.Identity, [r_xs[sl], r_ss], [r_u], scale=ss[:, 1:2])
                for kc in range(8):
                    tr(pT[0][:, kc * 128:(kc + 1) * 128], u[:, kc * 128:(kc + 1) * 128], [r_u], [r_pT[0]])
                cp("dve", uT[:].rearrange("p a b -> p (a b)"), pT[0][:], [r_pT[0]], [r_uT])

            def ingroup(c, gi):
                sl = c % 2
                pb = gcntA[0] % 2
                gcntA[0] += 1
                for kc in range(8):
                    mm(pA[pb][:], uT[:, kc, :], w0[:, kc, gi * 512:(gi + 1) * 512], kc == 0, kc == 7,
                       [r_uT, r_w0], [r_pA[pb]])
                if gi < 4:
                    pv = pA[pb][:].rearrange("p (h t d) -> p h t d", h=2, t=2)
                    ov = qk[:, gi * 512:(gi + 1) * 512].rearrange("p (h t d) -> p h t d", h=2, t=2)
                    csa = cs[:, sl, 0:2, :].unsqueeze(1).broadcast_to([128, 2, 2, 128])
                    csb = cs[:, sl, 2:4, :].unsqueeze(1).broadcast_to([128, 2, 2, 128])
                    ta, tb = rt[:, 0], rt[:, 1]
                    tt("dve", ta, pv, csa, ALU.mult, [r_pA[pb], r_cs[sl]], [r_rt[0]])
                    tt("dve", tb, pv, csb, ALU.mult, [r_pA[pb], r_cs[sl]], [r_rt[1]])
                    tt("pool", ov[:, :, 0, :], ta[:, :, 0, :], ta[:, :, 1, :], ALU.subtract, [r_rt[0]], [r_qk[gi]])
                    tt("pool", ov[:, :, 1, :], tb[:, :, 0, :], tb[:, :, 1, :], ALU.add, [r_rt[1]], [r_qk[gi]])
                elif gi < 8:
                    h = gi - 4
                    cp("act", v[:, sl, h * 512:(h + 1) * 512], pA[pb][:], [r_pA[pb]], [r_v[sl][h]])
                else:
                    h = gi - 8
                    s2 = h % 2
                    act(sg[:, s2, :], pA[pb][:], AF.Sigmoid, [r_pA[pb]], [r_sg[s2]])
                    tt("dve", gate[:, sl, h * 512:(h + 1) * 512], pA[pb][:], sg[:, s2, :], ALU.mult,
                       [r_pA[pb], r_sg[s2]], [r_gate[sl][h]])

            def ypre(c):
                for blk in range(8):
                    tr(pT[1][:, blk * 128:(blk + 1) * 128], qk[:, blk * 128:(blk + 1) * 128], [r_qk[blk // 4]], [r_pT[1]])
                tt("dve", qdT[:].rearrange("p (h t) i -> p h t i", t=2),
                   pT[1][:].rearrange("p (h t i) -> p h t i", h=4, t=2),
                   qdec[:].unsqueeze(2).broadcast_to([128, 4, 2, 128]), ALU.mult, [r_pT[1], r_qdec], [r_qdT])
                for blk in range(8):
                    tr(pT[0][:, blk * 128:(blk + 1) * 128], qk[:, 1024 + blk * 128:1024 + (blk + 1) * 128],
                       [r_qk[2 + blk // 4]], [r_pT[0]])
                cp("act", kT[:].rearrange("p a b -> p (a b)"), pT[0][:], [r_pT[0]], [r_kT])
                for h in range(4):
                    act(kd[:, h * 256:(h + 1) * 256], qk[:, 1024 + h * 256:1024 + (h + 1) * 256], AF.Identity,
                        [r_qk[2 + h // 2], r_kdec], [r_kd], scale=kdec[:, h:h + 1])

            GORDER = [0, 2, 1, 3, 4, 8, 5, 9, 6, 10, 7, 11]
            loadA(0)
            loadA(1)
            xpre(0)
            for gi in GORDER:
                ingroup(0, gi)
            for c in range(NCH):
                sl = c % 2
                osl = c % 2
                nxt = c + 1 < NCH
                if nxt:
                    xpre(c + 1)
                if c + 2 < NCH:
                    loadA(c + 2)
                ypre(c)
                for h in range(4):
                    s2 = h % 2
                    vh = v[:, sl, h * 512:(h + 1) * 512]
                    for t in range(2):
                        mm(pST[:, 0:128], kT[:, h * 2 + t, :], qdT[:, h * 2 + t, :], t == 0, t == 1,
                           [r_kT, r_qdT], [r_pST])
                    tt("dve", stm[:, s2, :], pST[:, 0:128], dtm[:, h, :], ALU.mult, [r_pST, r_dtm], [r_stm[s2]])
                    for t in range(2):
                        mm(pDS[t][:], kd[:, h * 256 + t * 128:h * 256 + (t + 1) * 128], vh, True, True,
                           [r_kd, r_v[sl][h]], [r_pDS[t]])
                    if nxt:
                        ingroup(c + 1, GORDER[3 * h])
                    mm(pO[:], stm[:, s2, :], vh, True, c == 0, [r_stm[s2], r_v[sl][h]], [r_pO])
                    if c > 0:
                        for t in range(2):
                            mm(pO[:], qdT[:, h * 2 + t, :], Sb[:, h, t, :], False, t == 1,
                               [r_qdT, r_Sb[h][t]], [r_pO])
                    for t in range(2):
                        if c == 0:
                            cp("dve", Sst[:, h, t, :], pDS[t][:], [r_pDS[t]], [r_S[h][t]])
                        else:
                            stt(Sst[:, h, t, :], Sst[:, h, t, :], _CDEC[h], pDS[t][:], ALU.mult, ALU.add,
                                [r_pDS[t], r_S[h][t]], [r_S[h][t]])
                        if nxt:
                            cp("act", Sb[:, h, t, :], Sst[:, h, t, :], [r_S[h][t]], [r_Sb[h][t]])
                    S.op("dve", lambda g, s2=s2: g.bn_stats(out=bn6[:, s2, :], in_=pO[:]), reads=[r_pO], writes=[r_bn[s2]])
                    S.op("dve", lambda g, s2=s2: g.bn_aggr(out=bn2[:, s2, 0:2], in_=bn6[:, s2, :]), reads=[r_bn[s2]], writes=[r_bn[s2]])
                    ts("dve", bn2[:, s2, 2:3], bn2[:, s2, 1:2], EPS, None, ALU.add, None, [r_bn[s2]], [r_bn[s2]])
                    tt("pool", bn2[:, s2, 3:4], bn2[:, s2, 2:3], nhalf[:], ALU.pow, [r_bn[s2], r_nhalf], [r_bn[s2]])
                    stt(nt[:, s2, :], pO[:], bn2[:, s2, 0:1], gate[:, sl, h * 512:(h + 1) * 512], ALU.subtract, ALU.mult,
                        [r_pO, r_bn[s2], r_gate[sl][h]], [r_nt[s2]])
                    act(og[:, osl, h * 512:(h + 1) * 512], nt[:, s2, :], AF.Identity, [r_nt[s2], r_bn[s2]], [r_og[osl]],
                        scale=bn2[:, s2, 3:4])
                    if nxt:
                        ingroup(c + 1, GORDER[3 * h + 1])
                        ingroup(c + 1, GORDER[3 * h + 2])
                dma(og_scr[c * 128:(c + 1) * 128, :], og[:, osl, :], [r_og[osl]], [r_og_scr[c]])
            S.barrier()

        with ExitStack() as pb_:
            stg = sbt(pb_, "stgB", [128, 2, 1024]); r_stg = [R(), R()]
            wo0, r_wo0 = load_w(pb_, "wo0", w_out0, 16, D, stg, r_stg, lambda c: None)
            wkv, r_wkv = load_w(pb_, "wkv", w_kv, 8, 3072, stg, r_stg, lambda c: 1)
            gp0 = sbt(pb_, "gp0", [128, D]); r_gp0 = R()
            dma(gp0[:], gpost[0:1, :].partition_broadcast(128), [], [r_gp0])
            KTs = sbt(pb_, "KTs", [128, 8, T], BF16); r_KTs = R()
            xs = sbt(pb_, "xsB", [128, 2, D]); r_xs = [R(), R()]
            ogc = sbt(pb_, "ogcB", [128, 2, 2048], BF16); r_ogc = [R(), R()]
            ogT = sbt(pb_, "ogTB", [128, 16, 128], BF16); r_ogT = R()
            tmp = sbt(pb_, "tmpB", [128, D]); r_tmp = R()
            junk = sbt(pb_, "junkB", [128, D], BF16); r_junk = R()
            un = sbt(pb_, "unB", [128, 2, D], BF16); r_un = [R(), R()]
            unT = sbt(pb_, "unTB", [128, 8, 128], BF16); r_unT = R()
            Kr = sbt(pb_, "KrB", [128, 1024], BF16); r_Kr = R()
            Vr = sbt(pb_, "VrB", [128, 2, 2048], BF16); r_Vr = [R(), R()]
            pY = [pst(pb_, f"pY{i}", [128, 512]) for i in range(2)]; r_pY = [R(), R()]
            pT = [pst(pb_, f"pTB{i}", [128, 1024], BF16) for i in range(2)]; r_pT = [R(), R()]
            pR = [pst(pb_, f"pRB{i}", [128, 512]) for i in range(2)]; r_pR = [R(), R()]
            pP = [pst(pb_, f"pPB{i}", [128, 512]) for i in range(2)]; r_pP = [R(), R()]

            def loadB(c):
                sl = c % 2
                dma(xs[:, sl, :], x[c * 128:(c + 1) * 128, :], [], [r_xs[sl]])
                dma(ogc[:, sl, :], og_scr[c * 128:(c + 1) * 128, :], [r_og_scr[c]], [r_ogc[sl]])

            gcntB = [0]

            def x_tr(c):
                sl = c % 2
                for hb in range(2):
                    for blk in range(8):
                        wc = hb * 8 + blk
                        tr(pT[hb][:, blk * 128:(blk + 1) * 128], ogc[:, sl, wc * 128:(wc + 1) * 128], [r_ogc[sl]], [r_pT[hb]])
                    cp("act" if hb == 0 else "dve", ogT[:, hb * 8:(hb + 1) * 8, :].rearrange("p a b -> p (a b)"),
                       pT[hb][:], [r_pT[hb]], [r_ogT])

            def x_y(c):
                sl = c % 2
                for cg in range(2):
                    for wc in range(16):
                        mm(pY[cg][:], ogT[:, wc, :], wo0[:, wc, cg * 512:(cg + 1) * 512], wc == 0, wc == 15,
                           [r_ogT, r_wo0], [r_pY[cg]])
                for cg in range(2):
                    act(junk[:, cg * 512:(cg + 1) * 512], pY[cg][:], AF.Square, [r_pY[cg]], [r_junk, r_ss],
                        accum=ss[:, 2 + cg:3 + cg])
                tt("dve", ss[:, 0:1], ss[:, 2:3], ss[:, 3:4], ALU.add, [r_ss], [r_ss])
                rstd_from_ss(0, D, r_ss)
                for cg in range(2):
                    stt(tmp[:, cg * 512:(cg + 1) * 512], pY[cg][:], ss[:, 1:2], gp0[:, cg * 512:(cg + 1) * 512],
                        ALU.mult, ALU.mult, [r_pY[cg], r_ss, r_gp0], [r_tmp])
                tt("pool", xs[:, sl, :], xs[:, sl, :], tmp[:], ALU.add, [r_xs[sl], r_tmp], [r_xs[sl]])
                dma(h1_scr[c * 128:(c + 1) * 128, :], xs[:, sl, :], [r_xs[sl]], [r_h1_scr[c]])
                act(junk[:], xs[:, sl, :], AF.Square, [r_xs[sl]], [r_junk, r_ss], accum=ss[:, 4:5])
                rstd_from_ss(4, D, r_ss)
                act(un[:, sl, :], xs[:, sl, :], AF.Identity, [r_xs[sl], r_ss], [r_un[sl]], scale=ss[:, 5:6])

            def z_rev(c):
                sl = c % 2
                for kc in range(8):
                    mm(pR[kc // 4][:, (kc % 4) * 128:(kc % 4 + 1) * 128], un[:, sl, kc * 128:(kc + 1) * 128], jrev[:], True, True,
                       [r_un[sl], r_jrev], [r_pR[kc // 4]])
                for hb in range(2):
                    cp("act" if hb == 0 else "dve", unT[:, hb * 4:(hb + 1) * 4, :].rearrange("p a b -> p (a b)"),
                       pR[hb][:], [r_pR[hb]], [r_unT])

            def z_kv(c):
                sl = c % 2
                m0 = T - 128 * (c + 1)
                for gi in range(6):
                    pb = gcntB[0] % 2
                    gcntB[0] += 1
                    for kc in range(8):
                        mm(pP[pb][:], unT[:, kc, :], wkv[:, kc, gi * 512:(gi + 1) * 512], kc == 0, kc == 7,
                           [r_unT, r_wkv], [r_pP[pb]])
                    if gi < 2:
                        cp("act", Kr[:, gi * 512:(gi + 1) * 512], pP[pb][:], [r_pP[pb]], [r_Kr])
                    else:
                        cp("dve" if gi % 2 else "act", Vr[:, sl, (gi - 2) * 512:(gi - 1) * 512], pP[pb][:], [r_pP[pb]], [r_Vr[sl]])
                for h in range(8):
                    tr(pT[0][:, h * 128:(h + 1) * 128], Kr[:, h * 128:(h + 1) * 128], [r_Kr], [r_pT[0]])
                cp("dve", KTs[:, :, m0:m0 + 128], pT[0][:].rearrange("p (h m) -> p h m", h=8), [r_pT[0]], [r_KTs])
                dma(v_scr[m0:m0 + 128, :], Vr[:, sl, :], [r_Vr[sl]], [r_v_scr])

            loadB(0)
            loadB(1)
            x_tr(0)
            x_y(0)
            for c in range(NCH):
                if c + 2 < NCH:
                    loadB(c + 2)
                if c + 1 < NCH:
                    x_tr(c + 1)
                z_rev(c)
                if c + 1 < NCH:
                    x_y(c + 1)
                z_kv(c)
            for h in range(8):
                dma(kT_scr[h], KTs[:, h, :], [r_KTs], [r_kT_scr[h]])
            S.barrier()

        with ExitStack() as pc:
            gp1 = sbt(pc, "gp1", [128, D]); r_gp1 = R()
            dma(gp1[:], gpost[1:2, :].partition_broadcast(128), [], [r_gp1])
            msk = sbt(pc, "msk", [128, 2, 256]); r_msk = R()
            dma(msk[:], masks_d, [], [r_msk])
            QT = sbt(pc, "QT", [128, 8, 2048], BF16); r_QT = [R() for _ in range(16)]
            gog = sbt(pc, "gog", [128, 16, 2048], BF16); r_gog = [[R() for _ in range(8)] for _ in range(16)]
            hb_ = sbt(pc, "hbC", [128, 2, D]); r_hb = [R(), R()]
            junk = sbt(pc, "junkC", [128, D], BF16); r_junk = R()

            def load_h1(s, sl):
                j, isB = s // 2, s % 2

                def f(g):
                    if isB == 0:
                        src = h1_scr[512 * j:512 * j + 256, :][bass.ds(regs["r128"], 128), :]
                    else:
                        src = h1_scr[512 * j + 256:512 * j + 512, :][bass.ds(regs["n128"], 128), :]
                    return g.dma_start(out=hb_[:, sl, :], in_=src)
                S.op("sp", f, reads=r_h1_scr[4 * j:4 * j + 4], writes=[r_hb[sl]], dma=True)

            with ExitStack() as p1:
                stg = sbt(p1, "stgC", [128, 2, 1024]); r_stg = [R(), R()]
                w1, r_w1 = load_w(p1, "w1", w_in1, 8, 3072, stg, r_stg, lambda c: 2)
                u = sbt(p1, "uC", [128, 2, D], BF16); r_u = [R(), R()]
                uT = sbt(p1, "uTC", [128, 2, 8, 128], BF16); r_uT = [R(), R()]
                Qt = sbt(p1, "QtC", [128, 1024], BF16); r_Qt = R()
                sg = sbt(p1, "sgC", [128, 2, 512]); r_sg = [R(), R()]
                pA = [pst(p1, f"pAC{i}", [128, 512]) for i in range(2)]; r_pA = [R(), R()]
                pT = [pst(p1, f"pTC{i}", [128, 1024], BF16) for i in range(2)]; r_pT = [R(), R()]
                gcntC = [0]

                def p_pre(s):
                    sl = s % 2
                    act(junk[:], hb_[:, sl, :], AF.Square, [r_hb[sl]], [r_junk, r_ss], accum=ss[:, 0:1])
                    rstd_from_ss(0, D, r_ss)
                    act(u[:, sl, :], hb_[:, sl, :], AF.Identity, [r_hb[sl], r_ss], [r_u[sl]], scale=ss[:, 1:2])
                    for kc in range(8):
                        tr(pT[0][:, kc * 128:(kc + 1) * 128], u[:, sl, kc * 128:(kc + 1) * 128], [r_u[sl]], [r_pT[0]])
                    cp("dve", uT[:, sl].rearrange("p a b -> p (a b)"), pT[0][:], [r_pT[0]], [r_uT[sl]])

                def p_main(s):
                    sl = s % 2
                    for gi in range(6):
                        pb = gcntC[0] % 2
                        gcntC[0] += 1
                        for kc in range(8):
                            mm(pA[pb][:], uT[:, sl, kc, :], w1[:, kc, gi * 512:(gi + 1) * 512], kc == 0, kc == 7,
                               [r_uT[sl], r_w1], [r_pA[pb]])
                        if gi < 2:
                            cp("act", Qt[:, gi * 512:(gi + 1) * 512], pA[pb][:], [r_pA[pb]], [r_Qt])
                        else:
                            k4 = gi - 2
                            s2 = k4 % 2
                            act(sg[:, s2, :], pA[pb][:], AF.Sigmoid, [r_pA[pb]], [r_sg[s2]])
                            tt("dve", gog[:, s, k4 * 512:(k4 + 1) * 512], pA[pb][:], sg[:, s2, :], ALU.mult,
                               [r_pA[pb], r_sg[s2]], [r_gog[s][2 * k4], r_gog[s][2 * k4 + 1]])
                    for h in range(8):
                        tr(pT[1][:, h * 128:(h + 1) * 128], Qt[:, h * 128:(h + 1) * 128], [r_Qt], [r_pT[1]])
                    cp("act", QT[:, :, s * 128:(s + 1) * 128], pT[1][:].rearrange("p (h m) -> p h m", h=8),
                       [r_pT[1]], [r_QT[s]])

                load_h1(0, 0)
                load_h1(1, 1)
                p_pre(0)
                for s in range(16):
                    if s + 1 < 16:
                        p_pre(s + 1)
                    if s + 2 < 16:
                        load_h1(s + 2, s % 2)
                    p_main(s)
                S.barrier()

            with ExitStack() as p2:
                NS = 4
                KTh = sbt(p2, "KTh", [128, 2, T], BF16); r_KTh = [R(), R()]
                Vh = sbt(p2, "Vh", [128, 2, 32, 256], BF16); r_Vh = [R(), R()]
                Pt = sbt(p2, "Pt", [128, NS, 512]); r_Pt = [R() for _ in range(NS)]
                Cb = sbt(p2, "Cb", [128, NS, 513]); r_Cb = [R() for _ in range(NS)]; r_Cb0 = [R() for _ in range(NS)]
                Wt = sbt(p2, "Wt", [128, NS, 512], BF16)
                r_WtA = [R() for _ in range(NS)]; r_WtB = [R() for _ in range(NS)]
                WTs = sbt(p2, "WTs", [128, NS, 512], BF16); r_WTs = [R() for _ in range(NS)]
                pZ = [pst(p2, f"pZ{i}", [128, 512]) for i in range(2)]; r_pZ = [R(), R()]
                pW = [pst(p2, f"pW{i}", [128, 1024], BF16) for i in range(2)]; r_pW = [R(), R()]
                pOa = [pst(p2, f"pOa{i}", [128, 512]) for i in range(2)]; r_pOa = [R(), R()]
                v_view = v_scr.rearrange("(mb p) w -> p mb w", p=128)
                scale = -1.0 / math.sqrt(128.0)

                def loadKV(h):
                    sl = h % 2
                    dma(KTh[:, sl, :], kT_scr[h], [r_kT_scr[h]], [r_KTh[sl]])
                    for q4 in range(4):
                        dma(Vh[:, sl, q4 * 8:(q4 + 1) * 8, :], v_view[:, q4 * 8:(q4 + 1) * 8, h * 256:(h + 1) * 256],
                            [r_v_scr], [r_Vh[sl]])

                tl = []
                ocnt = 0
                for h in range(8):
                    for s in range(16):
                        j, isB = s // 2, s % 2
                        nkb = 4 * j + 2 + 2 * isB
                        mb = 32 - nkb
                        tiles = []
                        if nkb % 4 == 2:
                            tiles.append((mb, 2))
                            mb += 2
                        while mb < 32:
                            tiles.append((mb, 4))
                            mb += 4
                        for ti, (tmb, nb) in enumerate(tiles):
                            tl.append(dict(h=h, s=s, isB=isB, ti=ti, nt=len(tiles), tmb=tmb, nb=nb, ob=ocnt % 2,
                                           newh=(s == 0 and ti == 0)))
                        ocnt += 1

                def st1(i):
                    t = tl[i]
                    h, s, w, m0_ = t["h"], t["s"], t["nb"] * 128, t["tmb"] * 128
                    ksl = h % 2
                    k = i % NS
                    z = i % 2
                    mm(pZ[z][:, 0:w], QT[:, h, s * 128:(s + 1) * 128], KTh[:, ksl, m0_:m0_ + w], True, True,
                       [r_QT[s], r_KTh[ksl]], [r_pZ[z]])
                    act(Pt[:, k, 0:w], pZ[z][:, 0:w], AF.Sigmoid, [r_pZ[z]], [r_Pt[k]], scale=scale)
                    if t["ti"] == 0:
                        tt("dve", Pt[:, k, 0:256], Pt[:, k, 0:256], msk[:, t["isB"], :], ALU.max,
                           [r_Pt[k], r_msk], [r_Pt[k]])
                        S.op("pool", lambda g: g.memset(Cb[:, k, 0:1], 1.0), writes=[r_Cb0[k]])
                        S.op("dve", lambda g: g.tensor_tensor_scan(
                            out=Cb[:, k, 1:w + 1], data0=Pt[:, k, 0:w], data1=zeros[:, 0:w], initial=1.0,
                            op0=ALU.mult, op1=ALU.add), reads=[r_Pt[k], r_zeros], writes=[r_Cb[k]])
                    else:
                        pk = (i - 1) % NS
                        pw = tl[i - 1]["nb"] * 128
                        cp("pool", Cb[:, k, 0:1], Cb[:, pk, pw:pw + 1], [r_Cb[pk]], [r_Cb0[k]])
                        S.op("dve", lambda g: g.tensor_tensor_scan(
                            out=Cb[:, k, 1:w + 1], data0=Pt[:, k, 0:w], data1=zeros[:, 0:w], initial=Cb[:, pk, pw:pw + 1],
                            op0=ALU.mult, op1=ALU.add), reads=[r_Pt[k], r_zeros, r_Cb[pk]], writes=[r_Cb[k]])
                    tt("pool", Wt[:, k, 0:w], Cb[:, k, 0:w], Cb[:, k, 1:w + 1], ALU.subtract, [r_Cb[k], r_Cb0[k]], [r_WtB[k]])

                def st2(i):
                    t = tl[i]
                    w = t["nb"] * 128
                    k = i % NS
                    z = i % 2
                    for b4 in range(t["nb"]):
                        tr(pW[z][:, b4 * 128:(b4 + 1) * 128], Wt[:, k, b4 * 128:(b4 + 1) * 128],
                           [r_WtB[k]], [r_pW[z]])
                    cp("act", WTs[:, k, 0:w], pW[z][:, 0:w], [r_pW[z]], [r_WTs[k]])

                def st3(i):
                    t = tl[i]
                    h, s, ob = t["h"], t["s"], t["ob"]
                    ksl = h % 2
                    k = i % NS
                    if t["newh"] and h + 1 < 8:
                        loadKV(h + 1)
                    for b4 in range(t["nb"]):
                        first = (t["ti"] == 0 and b4 == 0)
                        last = (t["ti"] == t["nt"] - 1 and b4 == t["nb"] - 1)
                        mm(pOa[ob][:, 0:256], WTs[:, k, b4 * 128:(b4 + 1) * 128], Vh[:, ksl, t["tmb"] + b4, :], first, last,
                           [r_WTs[k], r_Vh[ksl]], [r_pOa[ob]])
                    if t["ti"] == t["nt"] - 1:
                        tt("dve", gog[:, s, h * 256:(h + 1) * 256], pOa[ob][:, 0:256], gog[:, s, h * 256:(h + 1) * 256], ALU.mult,
                           [r_pOa[ob], r_gog[s][h]], [r_gog[s][h]])

                loadKV(0)
                ntl = len(tl)
                for i in range(ntl + 3):
                    if i < ntl:
                        st1(i)
                    if 0 <= i - 2 < ntl:
                        st2(i - 2)
                    if 0 <= i - 3 < ntl:
                        st3(i - 3)
                S.barrier()

            with ExitStack() as p3:
                stg = sbt(p3, "stgE", [128, 2, 1024]); r_stg = [R(), R()]
                wo1, r_wo1 = load_w(p3, "wo1", w_out1, 16, D, stg, r_stg, lambda c: None)
                ogT = sbt(p3, "ogTE", [128, 2, 16, 128], BF16); r_ogT = [R(), R()]
                tmp = sbt(p3, "tmpE", [128, D]); r_tmp = R()
                ot = sbt(p3, "otE", [128, 2, D]); r_ot = [R(), R()]
                pY = [pst(p3, f"pYE{i}", [128, 512]) for i in range(2)]; r_pY = [R(), R()]
                pT = [pst(p3, f"pTE{i}", [128, 1024], BF16) for i in range(2)]; r_pT = [R(), R()]

                def e_tr(s):
                    sl = s % 2
                    for hb in range(2):
                        for blk in range(8):
                            wc = hb * 8 + blk
                            tr(pT[hb][:, blk * 128:(blk + 1) * 128], gog[:, s, wc * 128:(wc + 1) * 128],
                               [r_gog[s][wc // 2]], [r_pT[hb]])
                        cp("act" if hb == 0 else "dve", ogT[:, sl, hb * 8:(hb + 1) * 8, :].rearrange("p a b -> p (a b)"),
                           pT[hb][:], [r_pT[hb]], [r_ogT[sl]])

                def e_y(s):
                    sl = s % 2
                    for cg in range(2):
                        for wc in range(16):
                            mm(pY[cg][:], ogT[:, sl, wc, :], wo1[:, wc, cg * 512:(cg + 1) * 512], wc == 0, wc == 15,
                               [r_ogT[sl], r_wo1], [r_pY[cg]])
                    for cg in range(2):
                        act(junk[:, cg * 512:(cg + 1) * 512], pY[cg][:], AF.Square, [r_pY[cg]], [r_junk, r_ss],
                            accum=ss[:, 2 + cg:3 + cg])
                    tt("dve", ss[:, 0:1], ss[:, 2:3], ss[:, 3:4], ALU.add, [r_ss], [r_ss])
                    rstd_from_ss(0, D, r_ss)
                    for cg in range(2):
                        stt(tmp[:, cg * 512:(cg + 1) * 512], pY[cg][:], ss[:, 1:2], gp1[:, cg * 512:(cg + 1) * 512],
                            ALU.mult, ALU.mult, [r_pY[cg], r_ss, r_gp1], [r_tmp])
                    tt("pool", ot[:, sl, :], hb_[:, sl, :], tmp[:], ALU.add, [r_hb[sl], r_tmp], [r_ot[sl]])
                    dma(out[s * 128:(s + 1) * 128, :], ot[:, sl, :], [r_ot[sl]], [r_out])

                load_h1(0, 0)
                load_h1(1, 1)
                e_tr(0)
                for s in range(16):
                    if s + 1 < 16:
                        e_tr(s + 1)
                    e_y(s)
                    if s + 2 < 16:
                        load_h1(s + 2, s % 2)

        with nc.Block() as block:
            S.emit(block, esem, dsem)
    return nc


_NC = None


def _layout_inputs(x, ret_norm_pre, ret_w_in, ret_w_out, ret_norm_post, kv_norm, w_kv,
                   sb_norm_pre, sb_w_in, sb_w_out, sb_norm_post):
    cossin, dtm, qdec, kdec, _ = _host_consts()
    f = lambda a: np.ascontiguousarray(np.asarray(a, dtype=np.float32))
    x = f(x)

    def g8(gv):
        return np.asarray(gv, np.float32).reshape(8, 128).T
    gpre = np.ascontiguousarray(np.stack([g8(ret_norm_pre[0]), g8(kv_norm), g8(sb_norm_pre[0])], axis=1))
    gpost = np.ascontiguousarray(np.stack([np.asarray(ret_norm_post[0], np.float32),
                                           np.asarray(sb_norm_post[0], np.float32)], axis=0))
    common = {
        "w_in0": f(ret_w_in[0]), "w_out0": f(ret_w_out[0]), "w_kv": f(w_kv), "w_in1": f(sb_w_in[0]),
        "w_out1": f(sb_w_out[0]), "gpre": gpre, "gpost": gpost, "cossin": cossin, "dtm": dtm, "qdec": qdec,
        "kdec": kdec,
    }
    maps = []
    for c in range(8):
        b, r = c // 2, c % 2
        m = dict(common)
        m["x"] = x[b]
        m["masks"] = _masks(r)
        m["rid"] = np.array([[r, 1 - r]], np.int32)
        maps.append(m)
    return maps


def _slot_block(s, r):
    j, isB = s // 2, s % 2
    return 4 * j + r if isB == 0 else 4 * j + 3 - r


def kernel(**inputs):
    global _NC
    if _NC is None:
        _NC = build_nc()
    maps = _layout_inputs(**inputs)
    res = run_bass_kernel_spmd(_NC, maps, core_ids=list(range(8)))
    outp = np.empty((4, T, D), np.float32)
    for c in range(8):
        b, r = c // 2, c % 2
        o = res.results[c]["out"]
        for s in range(16):
            qb = _slot_block(s, r)
            outp[b, qb * 128:(qb + 1) * 128, :] = o[s * 128:(s + 1) * 128, :]
    return outp
```

```python
import math
from contextlib import ExitStack

import numpy as np
import concourse.bass as bass
import concourse.mybir as mybir
from concourse.bass_utils import run_bass_kernel_spmd

F32 = mybir.dt.float32
BF16 = mybir.dt.bfloat16
I32 = mybir.dt.int32
AF = mybir.ActivationFunctionType
ALU = mybir.AluOpType

T = 4096
D = 1024
NCH = 32
EPS = 1e-6
ENGS = ("pe", "act", "dve", "pool", "sp")
NDMA = 24


class Res:
    __slots__ = ("w", "rs")

    def __init__(self):
        self.w = None
        self.rs = []


class Op:
    __slots__ = ("eng", "idx", "fn", "waits", "flag", "dma", "val", "pre")

    def __init__(self, eng, idx, fn):
        self.eng = eng
        self.idx = idx
        self.fn = fn
        self.waits = []
        self.flag = False
        self.dma = None
        self.val = None
        self.pre = None


class Sched:
    def __init__(self, ndma=NDMA):
        self.q = {e: [] for e in ENGS}
        self.seen = {e: {} for e in ENGS}
        self.ndma = ndma
        self.dma_n = {e: 0 for e in ENGS}
        self.dma_last = {}
        self.last = {e: None for e in ENGS}

    def _add_wait(self, op, prod):
        if prod is None or prod is op:
            return
        if prod.dma is None:
            if prod.eng == op.eng and op.eng == "pe" and op.dma is None:
                return
            key = prod.eng
        else:
            key = ("d", prod.eng, prod.dma[0])
        s = self.seen[op.eng]
        if s.get(key, -1) >= prod.idx:
            return
        s[key] = prod.idx
        prod.flag = True
        op.waits.append(prod)

    def op(self, eng, fn, reads=(), writes=(), dma=False, after=()):
        q = self.q[eng]
        o = Op(eng, len(q), fn)
        if dma:
            n = self.dma_n[eng]
            slot = n % self.ndma
            o.dma = (slot, 16 * (n // self.ndma + 1))
            self.dma_n[eng] = n + 1
            o.pre = self.dma_last.get((eng, slot))
            self.dma_last[(eng, slot)] = o
            o.idx = n
        for p in after:
            self._add_wait(o, p)
        for r in reads:
            self._add_wait(o, r.w)
        for r in writes:
            self._add_wait(o, r.w)
            for t in r.rs:
                self._add_wait(o, t)
        q.append(o)
        for r in reads:
            r.rs.append(o)
        for r in writes:
            r.w = o
            r.rs = []
        if not dma:
            self.last[eng] = o
        return o

    def barrier(self):
        prods = [o for o in self.last.values() if o is not None] + list(self.dma_last.values())
        for e in ENGS:
            self.op(e, lambda g: g.nop(), after=prods)

    def emit(self, block, esem, dsem):
        for e in ENGS:
            c = 0
            for o in self.q[e]:
                if o.dma is None and o.flag:
                    c += 1
                    o.val = c

        def tok(p):
            if p.dma is None:
                return esem[p.eng], p.val
            return dsem[p.eng][p.dma[0]], p.dma[1]

        def run(e, g):
            for o in self.q[e]:
                if o.pre is not None:
                    s, v = tok(o.pre)
                    g.wait_ge(s, v)
                for p in o.waits:
                    s, v = tok(p)
                    g.wait_ge(s, v)
                ins = o.fn(g)
                if o.dma is not None:
                    ins.then_inc(dsem[e][o.dma[0]], 16)
                elif o.flag:
                    ins.then_inc(esem[e], 1)
            for (qe, _slot), o in self.dma_last.items():
                if qe == e:
                    s, v = tok(o)
                    g.wait_ge(s, v)

        @block.tensor
        def _(g):
            run("pe", g)

        @block.scalar
        def _(g):
            run("act", g)

        @block.vector
        def _(g):
            run("dve", g)

        @block.gpsimd
        def _(g):
            run("pool", g)

        @block.sync
        def _(g):
            run("sp", g)


def _host_consts():
    half = 128
    inv_freq = (1.0 / (np.float32(10000.0) ** (np.arange(half, dtype=np.float32) / np.float32(half)))).astype(np.float32)
    pos = np.arange(T, dtype=np.float32)
    ang = (pos[:, None] * inv_freq[None, :]).astype(np.float32)
    cossin = np.stack([np.cos(ang), np.sin(ang)], axis=1).astype(np.float32)
    lg = np.log1p(-np.exp2(-5.0 - np.arange(4, dtype=np.float64)))
    idx = np.arange(128, dtype=np.float64)
    dtm = np.zeros((128, 4, 128), np.float64)
    for h in range(4):
        dj = np.exp(-lg[h] * (idx + 1.0))
        dtm[:, h, :] = np.where(idx[:, None] <= idx[None, :], dj[:, None], 0.0)
    qdec = np.broadcast_to(np.exp(lg[None, :, None] * (idx[None, None, :] + 1.0)), (128, 4, 128))
    kdec = np.exp(lg[None, :] * (127.0 - idx[:, None]))
    cdec = [float(np.exp(lg[h] * 128.0)) for h in range(4)]
    return (cossin, dtm.astype(np.float32), np.ascontiguousarray(qdec).astype(np.float32),
            kdec.astype(np.float32), cdec)


def _masks(r):
    i = np.arange(128)[:, None]
    m = np.arange(128)[None, :]
    diag = (m <= 127 - i).astype(np.float32)
    full = np.ones((128, 128), np.float32)
    none = np.zeros((128, 128), np.float32)
    if r == 0:
        a = np.concatenate([full, diag], 1)
        b = np.concatenate([diag, none], 1)
    else:
        a = np.concatenate([diag, none], 1)
        b = np.concatenate([full, diag], 1)
    return np.stack([a, b], axis=1).astype(np.float32)


_CDEC = _host_consts()[4]


def build_nc(dbg=False):
    nc = bass.Bass("TRN2", target_bir_lowering=False)
    okind = "ExternalOutput" if dbg else "Internal"

    def din(name, shape, dt=F32):
        return nc.dram_tensor(name, shape, dt, kind="ExternalInput").ap()

    x = din("x", [T, D])
    w_in0 = din("w_in0", [D, 6144])
    w_out0 = din("w_out0", [2048, D])
    w_kv = din("w_kv", [D, 3072])
    w_in1 = din("w_in1", [D, 3072])
    w_out1 = din("w_out1", [2048, D])
    gpre = din("gpre", [128, 3, 8])
    gpost = din("gpost", [2, D])
    cossin = din("cossin", [T, 2, 128])
    dtm_d = din("dtm", [128, 4, 128])
    qdec_d = din("qdec", [128, 4, 128])
    kdec_d = din("kdec", [128, 4])
    masks_d = din("masks", [128, 2, 256])
    rid = din("rid", [1, 2], I32)
    out = nc.dram_tensor("out", [2048, D], F32, kind="ExternalOutput").ap()
    og_scr = nc.dram_tensor("og_scr", [T, 2048], BF16, kind=okind).ap()
    h1_scr = nc.dram_tensor("h1_scr", [T, D], F32, kind=okind).ap()
    kT_scr = nc.dram_tensor("kT_scr", [8, 128, T], BF16, kind=okind).ap()
    v_scr = nc.dram_tensor("v_scr", [T, 2048], BF16, kind=okind).ap()

    S = Sched()
    R = Res
    r_og_scr = [R() for _ in range(NCH)]
    r_h1_scr = [R() for _ in range(NCH)]
    r_kT_scr = [R() for _ in range(8)]
    r_v_scr = R()
    r_out = R()

    with ExitStack() as top:
        esem = {e: top.enter_context(nc.semaphore("es_" + e)) for e in ENGS}
        dsem = {"sp": [top.enter_context(nc.semaphore(f"ds_{i}")) for i in range(NDMA)]}
        reg0 = top.enter_context(nc.sync.register("reg0"))
        reg1 = top.enter_context(nc.sync.register("reg1"))
        regs = {}

        def sbt(st, name, shape, dt=F32):
            return st.enter_context(nc.sbuf_tensor(name, shape, dt))

        def pst(st, name, shape, dt=F32):
            return st.enter_context(nc.psum_tensor(name, shape, dt))

        ident = sbt(top, "ident", [128, 128], BF16); r_ident = R()
        jrev = sbt(top, "jrev", [128, 128], BF16); r_jrev = R()
        gpre_s = sbt(top, "gpre_s", [128, 4, 8]); r_gpre = R()
        ss = sbt(top, "ss", [128, 8]); r_ss = R()
        nhalf = sbt(top, "nhalf", [128, 1]); r_nhalf = R()
        zeros = sbt(top, "zeros", [128, 512]); r_zeros = R()

        def dma(out_ap, in_ap, reads, writes):
            return S.op("sp", lambda g: g.dma_start(out=out_ap, in_=in_ap), reads=reads, writes=writes, dma=True)

        def mm(out_ap, lhsT, rhs, start, stop, reads, writes):
            return S.op("pe", lambda g: g.matmul(out_ap, lhsT=lhsT, rhs=rhs, start=start, stop=stop),
                        reads=reads, writes=writes)

        def tr(out_ap, in_ap, reads, writes):
            return S.op("pe", lambda g: g.transpose(out=out_ap, in_=in_ap, identity=ident[:]),
                        reads=list(reads) + [r_ident], writes=writes)

        def act(out_ap, in_ap, func, reads, writes, scale=None, accum=None):
            kw = {}
            if scale is not None:
                kw["scale"] = scale
            if accum is not None:
                kw["accum_out"] = accum
            return S.op("act", lambda g: g.activation(out=out_ap, in_=in_ap, func=func, **kw),
                        reads=reads, writes=writes)

        def tt(eng, out_ap, in0, in1, op, reads, writes):
            return S.op(eng, lambda g: g.tensor_tensor(out=out_ap, in0=in0, in1=in1, op=op), reads=reads, writes=writes)

        def ts(eng, out_ap, in0, s1, s2, op0, op1, reads, writes):
            if s2 is None:
                return S.op(eng, lambda g: g.tensor_scalar(out=out_ap, in0=in0, scalar1=s1, scalar2=None, op0=op0),
                            reads=reads, writes=writes)
            return S.op(eng, lambda g: g.tensor_scalar(out=out_ap, in0=in0, scalar1=s1, scalar2=s2, op0=op0, op1=op1),
                        reads=reads, writes=writes)

        def stt(out_ap, in0, scalar, in1, op0, op1, reads, writes):
            return S.op("dve", lambda g: g.scalar_tensor_tensor(out=out_ap, in0=in0, scalar=scalar, in1=in1,
                                                                 op0=op0, op1=op1), reads=reads, writes=writes)

        def cp(eng, out_ap, in_ap, reads, writes):
            if eng == "act":
                return act(out_ap, in_ap, AF.Copy, reads, writes)
            return S.op(eng, lambda g: g.tensor_copy(out=out_ap, in_=in_ap), reads=reads, writes=writes)

        def rstd_from_ss(col, n, r_src):
            ts("dve", ss[:, col + 1:col + 2], ss[:, col:col + 1], 1.0 / n, EPS, ALU.mult, ALU.add, [r_src], [r_src])
            tt("pool", ss[:, col + 1:col + 2], ss[:, col + 1:col + 2], nhalf[:], ALU.pow, [r_src, r_nhalf], [r_src])

        with ExitStack() as c0:
            onesf = sbt(c0, "onesf", [128, 128]); r_onesf = R()
            tmpf = sbt(c0, "tmpf", [128, 128]); r_tmpf = R()
            S.op("pool", lambda g: g.memset(onesf[:], 1.0), writes=[r_onesf])
            S.op("pool", lambda g: g.memset(nhalf[:], -0.5), writes=[r_nhalf])
            S.op("pool", lambda g: g.memset(zeros[:], 0.0), writes=[r_zeros])
            S.op("pool", lambda g: g.affine_select(out=tmpf[:], in_=onesf[:], pattern=[[-1, 128]], compare_op=ALU.is_equal,
                                                   fill=0.0, base=0, channel_multiplier=1), reads=[r_onesf], writes=[r_tmpf])
            cp("dve", ident[:], tmpf[:], [r_tmpf], [r_ident])
            S.op("pool", lambda g: g.affine_select(out=tmpf[:], in_=onesf[:], pattern=[[1, 128]], compare_op=ALU.is_equal,
                                                   fill=0.0, base=-127, channel_multiplier=1), reads=[r_onesf], writes=[r_tmpf])
            cp("dve", jrev[:], tmpf[:], [r_tmpf], [r_jrev])
            dma(gpre_s[:, 0:3, :], gpre, [], [r_gpre])
            ts("dve", gpre_s[:, 3, :], gpre_s[:, 0, :], 1.0 / 16.0, None, ALU.mult, None, [r_gpre], [r_gpre])

            def load_regs(g):
                g.reg_load(reg0, rid[0:1, 0:1])
                g.reg_load(reg1, rid[0:1, 1:2])
                regs["r"] = g.snap(reg0, min_val=0, max_val=1)
                regs["nr"] = g.snap(reg1, min_val=0, max_val=1)
                regs["r128"] = regs["r"] * 128
                regs["n128"] = regs["nr"] * 128
                return g.nop()
            S.op("sp", load_regs)
            S.barrier()

        rr = [0]

        def load_w(st, name, w_dram, nk, ncols, stg, r_stg, gcol_of_piece, piece=512):
            wt = sbt(st, name, [128, nk, ncols], BF16)
            r_w = R()
            for kc in range(nk):
                for c0_ in range(0, ncols, piece):
                    sl = rr[0] % 4
                    dma(stg[:, sl, 0:piece], w_dram[kc * 128:(kc + 1) * 128, c0_:c0_ + piece], [], [r_stg[sl]])
                    gi = gcol_of_piece(c0_)
                    eng = ("dve", "act")[rr[0] % 2]
                    o_ap = wt[:, kc, c0_:c0_ + piece]
                    i_ap = stg[:, sl, 0:piece]
                    if gi is None:
                        cp(eng, o_ap, i_ap, [r_stg[sl]], [r_w])
                    elif eng == "act":
                        act(o_ap, i_ap, AF.Identity, [r_stg[sl], r_gpre], [r_w], scale=gpre_s[:, gi, kc:kc + 1])
                    else:
                        ts(eng, o_ap, i_ap, gpre_s[:, gi, kc:kc + 1], None, ALU.mult, None, [r_stg[sl], r_gpre], [r_w])
                    rr[0] += 1
            return wt, r_w

        with ExitStack() as pa:
            stg = sbt(pa, "stgA", [128, 4, 512]); r_stg = [R(), R(), R(), R()]
            w0, r_w0 = load_w(pa, "w0", w_in0, 8, 6144, stg, r_stg,
                              lambda c: 3 if 1024 <= c < 2048 else 0)
            dtm = sbt(pa, "dtm_s", [128, 4, 128]); r_dtm = R()
            qdec = sbt(pa, "qdec_s", [128, 4, 128]); r_qdec = R()
            kdec = sbt(pa, "kdec_s", [128, 4]); r_kdec = R()
            dma(dtm[:], dtm_d, [], [r_dtm])
            dma(qdec[:], qdec_d, [], [r_qdec])
            dma(kdec[:], kdec_d, [], [r_kdec])
            Sst = sbt(pa, "Sst", [128, 4, 2, 512]); r_S = [[R(), R()] for _ in range(4)]
            Sb = sbt(pa, "Sb", [128, 4, 2, 512], BF16); r_Sb = [[R(), R()] for _ in range(4)]
            xs = sbt(pa, "xsA", [128, 2, D]); r_xs = [R(), R()]
            cs = sbt(pa, "csA", [128, 2, 2, 128]); r_cs = [R(), R()]
            junk = sbt(pa, "junkA", [128, D], BF16); r_junk = R()
            u = sbt(pa, "uA", [128, D], BF16); r_u = R()
            uT = sbt(pa, "uTA", [128, 8, 128], BF16); r_uT = R()
            qk = sbt(pa, "qkA", [128, 2048], BF16); r_qk = [R() for _ in range(4)]
            rt = sbt(pa, "rtA", [128, 4, 2, 128]); r_rt = [R() for _ in range(4)]
            qdT = sbt(pa, "qdTA", [128, 8, 128], BF16); r_qdT = R()
            kT = sbt(pa, "kTA", [128, 8, 128], BF16); r_kT = R()
            kd = sbt(pa, "kdA", [128, 1024], BF16); r_kd = R()
            v = sbt(pa, "vA", [128, 2, 2048], BF16); r_v = [[R() for _ in range(4)] for _ in range(2)]
            gate = sbt(pa, "gateA", [128, 2, 2048], BF16); r_gate = [[R() for _ in range(4)] for _ in range(2)]
            sg = sbt(pa, "sgA", [128, 2, 512]); r_sg = [R(), R()]
            stm = sbt(pa, "stmA", [128, 2, 128], BF16); r_stm = [R(), R()]
            og = sbt(pa, "ogA", [128, 2, 2048], BF16); r_og = [R(), R()]
            nt = sbt(pa, "ntA", [128, 2, 512]); r_nt = [R(), R()]
            bn6 = sbt(pa, "bn6A", [128, 2, 6]); bn2 = sbt(pa, "bn2A", [128, 2, 4]); r_bn = [R(), R()]
            pA = [pst(pa, f"pA{i}", [128, 512]) for i in range(2)]; r_pA = [R(), R()]
            pT = [pst(pa, f"pT{i}", [128, 1024], BF16) for i in range(2)]; r_pT = [R(), R()]
            pST = pst(pa, "pST", [128, 512]); r_pST = R()
            pO = pst(pa, "pO", [128, 512]); r_pO = R()
            pDS = [pst(pa, f"pDS{i}", [128, 512]) for i in range(2)]; r_pDS = [R(), R()]

            def loadA(c):
                sl = c % 2
                dma(xs[:, sl, :], x[c * 128:(c + 1) * 128, :], [], [r_xs[sl]])
                dma(cs[:, sl, :, :], cossin[c * 128:(c + 1) * 128, :, :], [], [r_cs[sl]])

            gcntA = [0]

            def xpre(c):
                sl = c % 2
                act(junk[:], xs[:, sl, :], AF.Square, [r_xs[sl]], [r_junk, r_ss], accum=ss[:, 0:1])
                rstd_from_ss(0, D, r_ss)
                act(u[:], xs[:, sl, :], AF.Identity, [r_xs[sl], r_ss], [r_u], scale=ss[:, 1:2])
                for kc in range(8):
                    tr(pT[0][:, kc * 128:(kc + 1) * 128], u[:, kc * 128:(kc + 1) * 128], [r_u], [r_pT[0]])
                cp("dve", uT[:].rearrange("p a b -> p (a b)"), pT[0][:], [r_pT[0]], [r_uT])

            def ingroup(c, gi):
                sl = c % 2
                pb = gcntA[0] % 2
                gcntA[0] += 1
                for kc in range(8):
                    mm(pA[pb][:], uT[:, kc, :], w0[:, kc, gi * 512:(gi + 1) * 512], kc == 0, kc == 7,
                       [r_uT, r_w0], [r_pA[pb]])
                if gi < 4:
                    pv = pA[pb][:].rearrange("p (h t d) -> p h t d", h=2, t=2)
                    ov = qk[:, gi * 512:(gi + 1) * 512].rearrange("p (h t d) -> p h t d", h=2, t=2)
                    cosb = cs[:, sl, 0, :].unsqueeze(1).broadcast_to([128, 2, 128])
                    sinb = cs[:, sl, 1, :].unsqueeze(1).broadcast_to([128, 2, 128])
                    ra, rb, rc_, rd = rt[:, 0], rt[:, 1], rt[:, 2], rt[:, 3]
                    tt("dve", ra, pv[:, :, 0, :], cosb, ALU.mult, [r_pA[pb], r_cs[sl]], [r_rt[0]])
                    tt("dve", rb, pv[:, :, 1, :], sinb, ALU.mult, [r_pA[pb], r_cs[sl]], [r_rt[1]])
                    tt("dve", rc_, pv[:, :, 0, :], sinb, ALU.mult, [r_pA[pb], r_cs[sl]], [r_rt[2]])
                    tt("dve", rd, pv[:, :, 1, :], cosb, ALU.mult, [r_pA[pb], r_cs[sl]], [r_rt[3]])
                    tt("pool", ov[:, :, 0, :], ra, rb, ALU.subtract, [r_rt[0], r_rt[1]], [r_qk[gi]])
                    tt("pool", ov[:, :, 1, :], rc_, rd, ALU.add, [r_rt[2], r_rt[3]], [r_qk[gi]])
                elif gi < 8:
                    h = gi - 4
                    cp("act", v[:, sl, h * 512:(h + 1) * 512], pA[pb][:], [r_pA[pb]], [r_v[sl][h]])
                else:
                    h = gi - 8
                    s2 = h % 2
                    act(sg[:, s2, :], pA[pb][:], AF.Sigmoid, [r_pA[pb]], [r_sg[s2]])
                    tt("dve", gate[:, sl, h * 512:(h + 1) * 512], pA[pb][:], sg[:, s2, :], ALU.mult,
                       [r_pA[pb], r_sg[s2]], [r_gate[sl][h]])

            def ypre(c):
                for blk in range(8):
                    tr(pT[1][:, blk * 128:(blk + 1) * 128], qk[:, blk * 128:(blk + 1) * 128], [r_qk[blk // 4]], [r_pT[1]])
                tt("dve", qdT[:].rearrange("p (h t) i -> p h t i", t=2),
                   pT[1][:].rearrange("p (h t i) -> p h t i", h=4, t=2),
                   qdec[:].unsqueeze(2).broadcast_to([128, 4, 2, 128]), ALU.mult, [r_pT[1], r_qdec], [r_qdT])
                for blk in range(8):
                    tr(pT[0][:, blk * 128:(blk + 1) * 128], qk[:, 1024 + blk * 128:1024 + (blk + 1) * 128],
                       [r_qk[2 + blk // 4]], [r_pT[0]])
                cp("act", kT[:].rearrange("p a b -> p (a b)"), pT[0][:], [r_pT[0]], [r_kT])
                for h in range(4):
                    act(kd[:, h * 256:(h + 1) * 256], qk[:, 1024 + h * 256:1024 + (h + 1) * 256], AF.Identity,
                        [r_qk[2 + h // 2], r_kdec], [r_kd], scale=kdec[:, h:h + 1])

            GORDER = [0, 2, 1, 3, 4, 8, 5, 9, 6, 10, 7, 11]
            loadA(0)
            loadA(1)
            xpre(0)
            for gi in GORDER:
                ingroup(0, gi)
            for c in range(NCH):
                sl = c % 2
                osl = c % 2
                nxt = c + 1 < NCH
                if nxt:
                    xpre(c + 1)
                if c + 2 < NCH:
                    loadA(c + 2)
                ypre(c)
                for h in range(4):
                    s2 = h % 2
                    vh = v[:, sl, h * 512:(h + 1) * 512]
                    for t in range(2):
                        mm(pST[:, 0:128], kT[:, h * 2 + t, :], qdT[:, h * 2 + t, :], t == 0, t == 1,
                           [r_kT, r_qdT], [r_pST])
                    tt("dve", stm[:, s2, :], pST[:, 0:128], dtm[:, h, :], ALU.mult, [r_pST, r_dtm], [r_stm[s2]])
                    for t in range(2):
                        mm(pDS[t][:], kd[:, h * 256 + t * 128:h * 256 + (t + 1) * 128], vh, True, True,
                           [r_kd, r_v[sl][h]], [r_pDS[t]])
                    if nxt:
                        ingroup(c + 1, GORDER[3 * h])
                    mm(pO[:], stm[:, s2, :], vh, True, c == 0, [r_stm[s2], r_v[sl][h]], [r_pO])
                    if c > 0:
                        for t in range(2):
                            mm(pO[:], qdT[:, h * 2 + t, :], Sb[:, h, t, :], False, t == 1,
                               [r_qdT, r_Sb[h][t]], [r_pO])
                    for t in range(2):
                        if c == 0:
                            cp("dve", Sst[:, h, t, :], pDS[t][:], [r_pDS[t]], [r_S[h][t]])
                        else:
                            stt(Sst[:, h, t, :], Sst[:, h, t, :], _CDEC[h], pDS[t][:], ALU.mult, ALU.add,
                                [r_pDS[t], r_S[h][t]], [r_S[h][t]])
                        if nxt:
                            cp("act", Sb[:, h, t, :], Sst[:, h, t, :], [r_S[h][t]], [r_Sb[h][t]])
                    S.op("dve", lambda g, s2=s2: g.bn_stats(out=bn6[:, s2, :], in_=pO[:]), reads=[r_pO], writes=[r_bn[s2]])
                    S.op("dve", lambda g, s2=s2: g.bn_aggr(out=bn2[:, s2, 0:2], in_=bn6[:, s2, :]), reads=[r_bn[s2]], writes=[r_bn[s2]])
                    ts("dve", bn2[:, s2, 2:3], bn2[:, s2, 1:2], EPS, None, ALU.add, None, [r_bn[s2]], [r_bn[s2]])
                    tt("pool", bn2[:, s2, 3:4], bn2[:, s2, 2:3], nhalf[:], ALU.pow, [r_bn[s2], r_nhalf], [r_bn[s2]])
                    stt(nt[:, s2, :], pO[:], bn2[:, s2, 0:1], gate[:, sl, h * 512:(h + 1) * 512], ALU.subtract, ALU.mult,
                        [r_pO, r_bn[s2], r_gate[sl][h]], [r_nt[s2]])
                    act(og[:, osl, h * 512:(h + 1) * 512], nt[:, s2, :], AF.Identity, [r_nt[s2], r_bn[s2]], [r_og[osl]],
                        scale=bn2[:, s2, 3:4])
                    if nxt:
                        ingroup(c + 1, GORDER[3 * h + 1])
                        ingroup(c + 1, GORDER[3 * h + 2])
                dma(og_scr[c * 128:(c + 1) * 128, :], og[:, osl, :], [r_og[osl]], [r_og_scr[c]])
            S.barrier()

        with ExitStack() as pb_:
            stg = sbt(pb_, "stgB", [128, 4, 512]); r_stg = [R(), R(), R(), R()]
            wo0, r_wo0 = load_w(pb_, "wo0", w_out0, 16, D, stg, r_stg, lambda c: None)
            wkv, r_wkv = load_w(pb_, "wkv", w_kv, 8, 3072, stg, r_stg, lambda c: 1)
            gp0 = sbt(pb_, "gp0", [128, D]); r_gp0 = R()
            dma(gp0[:], gpost[0:1, :].partition_broadcast(128), [], [r_gp0])
            KTs = sbt(pb_, "KTs", [128, 8, T], BF16); r_KTs = R()
            xs = sbt(pb_, "xsB", [128, 2, D]); r_xs = [R(), R()]
            ogc = sbt(pb_, "ogcB", [128, 2, 2048], BF16); r_ogc = [R(), R()]
            ogT = sbt(pb_, "ogTB", [128, 16, 128], BF16); r_ogT = R()
            tmp = sbt(pb_, "tmpB", [128, D]); r_tmp = R()
            junk = sbt(pb_, "junkB", [128, D], BF16); r_junk = R()
            un = sbt(pb_, "unB", [128, 2, D], BF16); r_un = [R(), R()]
            unT = sbt(pb_, "unTB", [128, 8, 128], BF16); r_unT = R()
            Kr = sbt(pb_, "KrB", [128, 1024], BF16); r_Kr = R()
            Vr = sbt(pb_, "VrB", [128, 2, 2048], BF16); r_Vr = [R(), R()]
            pY = [pst(pb_, f"pY{i}", [128, 512]) for i in range(2)]; r_pY = [R(), R()]
            pT = [pst(pb_, f"pTB{i}", [128, 1024], BF16) for i in range(2)]; r_pT = [R(), R()]
            pR = [pst(pb_, f"pRB{i}", [128, 512]) for i in range(2)]; r_pR = [R(), R()]
            pP = [pst(pb_, f"pPB{i}", [128, 512]) for i in range(2)]; r_pP = [R(), R()]

            def loadB(c):
                sl = c % 2
                dma(xs[:, sl, :], x[c * 128:(c + 1) * 128, :], [], [r_xs[sl]])
                dma(ogc[:, sl, :], og_scr[c * 128:(c + 1) * 128, :], [r_og_scr[c]], [r_ogc[sl]])

            gcntB = [0]

            def x_tr(c):
                sl = c % 2
                for hb in range(2):
                    for blk in range(8):
                        wc = hb * 8 + blk
                        tr(pT[hb][:, blk * 128:(blk + 1) * 128], ogc[:, sl, wc * 128:(wc + 1) * 128], [r_ogc[sl]], [r_pT[hb]])
                    cp("act" if hb == 0 else "dve", ogT[:, hb * 8:(hb + 1) * 8, :].rearrange("p a b -> p (a b)"),
                       pT[hb][:], [r_pT[hb]], [r_ogT])

            def x_y(c):
                sl = c % 2
                for cg in range(2):
                    for wc in range(16):
                        mm(pY[cg][:], ogT[:, wc, :], wo0[:, wc, cg * 512:(cg + 1) * 512], wc == 0, wc == 15,
                           [r_ogT, r_wo0], [r_pY[cg]])
                for cg in range(2):
                    act(junk[:, cg * 512:(cg + 1) * 512], pY[cg][:], AF.Square, [r_pY[cg]], [r_junk, r_ss],
                        accum=ss[:, 2 + cg:3 + cg])
                tt("dve", ss[:, 0:1], ss[:, 2:3], ss[:, 3:4], ALU.add, [r_ss], [r_ss])
                rstd_from_ss(0, D, r_ss)
                for cg in range(2):
                    stt(tmp[:, cg * 512:(cg + 1) * 512], pY[cg][:], ss[:, 1:2], gp0[:, cg * 512:(cg + 1) * 512],
                        ALU.mult, ALU.mult, [r_pY[cg], r_ss, r_gp0], [r_tmp])
                tt("pool", xs[:, sl, :], xs[:, sl, :], tmp[:], ALU.add, [r_xs[sl], r_tmp], [r_xs[sl]])
                dma(h1_scr[c * 128:(c + 1) * 128, :], xs[:, sl, :], [r_xs[sl]], [r_h1_scr[c]])
                act(junk[:], xs[:, sl, :], AF.Square, [r_xs[sl]], [r_junk, r_ss], accum=ss[:, 4:5])
                rstd_from_ss(4, D, r_ss)
                act(un[:, sl, :], xs[:, sl, :], AF.Identity, [r_xs[sl], r_ss], [r_un[sl]], scale=ss[:, 5:6])

            def z_rev(c):
                sl = c % 2
                for kc in range(8):
                    mm(pR[kc // 4][:, (kc % 4) * 128:(kc % 4 + 1) * 128], un[:, sl, kc * 128:(kc + 1) * 128], jrev[:], True, True,
                       [r_un[sl], r_jrev], [r_pR[kc // 4]])
                for hb in range(2):
                    cp("act" if hb == 0 else "dve", unT[:, hb * 4:(hb + 1) * 4, :].rearrange("p a b -> p (a b)"),
                       pR[hb][:], [r_pR[hb]], [r_unT])

            def z_kv(c):
                sl = c % 2
                m0 = T - 128 * (c + 1)
                for gi in range(6):
                    pb = gcntB[0] % 2
                    gcntB[0] += 1
                    for kc in range(8):
                        mm(pP[pb][:], unT[:, kc, :], wkv[:, kc, gi * 512:(gi + 1) * 512], kc == 0, kc == 7,
                           [r_unT, r_wkv], [r_pP[pb]])
                    if gi < 2:
                        cp("act", Kr[:, gi * 512:(gi + 1) * 512], pP[pb][:], [r_pP[pb]], [r_Kr])
                    else:
                        cp("dve" if gi % 2 else "act", Vr[:, sl, (gi - 2) * 512:(gi - 1) * 512], pP[pb][:], [r_pP[pb]], [r_Vr[sl]])
                for h in range(8):
                    tr(pT[0][:, h * 128:(h + 1) * 128], Kr[:, h * 128:(h + 1) * 128], [r_Kr], [r_pT[0]])
                cp("dve", KTs[:, :, m0:m0 + 128], pT[0][:].rearrange("p (h m) -> p h m", h=8), [r_pT[0]], [r_KTs])
                dma(v_scr[m0:m0 + 128, :], Vr[:, sl, :], [r_Vr[sl]], [r_v_scr])

            loadB(0)
            loadB(1)
            x_tr(0)
            x_y(0)
            for c in range(NCH):
                if c + 2 < NCH:
                    loadB(c + 2)
                if c + 1 < NCH:
                    x_tr(c + 1)
                z_rev(c)
                if c + 1 < NCH:
                    x_y(c + 1)
                z_kv(c)
            for h in range(8):
                dma(kT_scr[h], KTs[:, h, :], [r_KTs], [r_kT_scr[h]])
            S.barrier()

        with ExitStack() as pc:
            gp1 = sbt(pc, "gp1", [128, D]); r_gp1 = R()
            dma(gp1[:], gpost[1:2, :].partition_broadcast(128), [], [r_gp1])
            msk = sbt(pc, "msk", [128, 2, 256]); r_msk = R()
            dma(msk[:], masks_d, [], [r_msk])
            QT = sbt(pc, "QT", [128, 8, 2048], BF16); r_QT = [R() for _ in range(16)]
            gog = sbt(pc, "gog", [128, 16, 2048], BF16); r_gog = [[R() for _ in range(8)] for _ in range(16)]
            hb_ = sbt(pc, "hbC", [128, 2, D]); r_hb = [R(), R()]
            junk = sbt(pc, "junkC", [128, D], BF16); r_junk = R()

            def load_h1(s, sl):
                j, isB = s // 2, s % 2

                def f(g):
                    if isB == 0:
                        src = h1_scr[512 * j:512 * j + 256, :][bass.ds(regs["r128"], 128), :]
                    else:
                        src = h1_scr[512 * j + 256:512 * j + 512, :][bass.ds(regs["n128"], 128), :]
                    return g.dma_start(out=hb_[:, sl, :], in_=src)
                S.op("sp", f, reads=r_h1_scr[4 * j:4 * j + 4], writes=[r_hb[sl]], dma=True)

            with ExitStack() as p1:
                stg = sbt(p1, "stgC", [128, 4, 512]); r_stg = [R(), R(), R(), R()]
                w1, r_w1 = load_w(p1, "w1", w_in1, 8, 3072, stg, r_stg, lambda c: 2)
                u = sbt(p1, "uC", [128, 2, D], BF16); r_u = [R(), R()]
                uT = sbt(p1, "uTC", [128, 2, 8, 128], BF16); r_uT = [R(), R()]
                Qt = sbt(p1, "QtC", [128, 1024], BF16); r_Qt = R()
                sg = sbt(p1, "sgC", [128, 2, 512]); r_sg = [R(), R()]
                pA = [pst(p1, f"pAC{i}", [128, 512]) for i in range(2)]; r_pA = [R(), R()]
                pT = [pst(p1, f"pTC{i}", [128, 1024], BF16) for i in range(2)]; r_pT = [R(), R()]
                gcntC = [0]

                def p_pre(s):
                    sl = s % 2
                    act(junk[:], hb_[:, sl, :], AF.Square, [r_hb[sl]], [r_junk, r_ss], accum=ss[:, 0:1])
                    rstd_from_ss(0, D, r_ss)
                    act(u[:, sl, :], hb_[:, sl, :], AF.Identity, [r_hb[sl], r_ss], [r_u[sl]], scale=ss[:, 1:2])
                    for kc in range(8):
                        tr(pT[0][:, kc * 128:(kc + 1) * 128], u[:, sl, kc * 128:(kc + 1) * 128], [r_u[sl]], [r_pT[0]])
                    cp("dve", uT[:, sl].rearrange("p a b -> p (a b)"), pT[0][:], [r_pT[0]], [r_uT[sl]])

                def p_main(s):
                    sl = s % 2
                    for gi in range(6):
                        pb = gcntC[0] % 2
                        gcntC[0] += 1
                        for kc in range(8):
                            mm(pA[pb][:], uT[:, sl, kc, :], w1[:, kc, gi * 512:(gi + 1) * 512], kc == 0, kc == 7,
                               [r_uT[sl], r_w1], [r_pA[pb]])
                        if gi < 2:
                            cp("act", Qt[:, gi * 512:(gi + 1) * 512], pA[pb][:], [r_pA[pb]], [r_Qt])
                        else:
                            k4 = gi - 2
                            s2 = k4 % 2
                            act(sg[:, s2, :], pA[pb][:], AF.Sigmoid, [r_pA[pb]], [r_sg[s2]])
                            tt("dve", gog[:, s, k4 * 512:(k4 + 1) * 512], pA[pb][:], sg[:, s2, :], ALU.mult,
                               [r_pA[pb], r_sg[s2]], [r_gog[s][2 * k4], r_gog[s][2 * k4 + 1]])
                    for h in range(8):
                        tr(pT[1][:, h * 128:(h + 1) * 128], Qt[:, h * 128:(h + 1) * 128], [r_Qt], [r_pT[1]])
                    cp("act", QT[:, :, s * 128:(s + 1) * 128], pT[1][:].rearrange("p (h m) -> p h m", h=8),
                       [r_pT[1]], [r_QT[s]])

                load_h1(0, 0)
                load_h1(1, 1)
                p_pre(0)
                for s in range(16):
                    if s + 1 < 16:
                        p_pre(s + 1)
                    if s + 2 < 16:
                        load_h1(s + 2, s % 2)
                    p_main(s)
                S.barrier()

            with ExitStack() as p2:
                NS = 5
                KTh = sbt(p2, "KTh", [128, 2, T], BF16); r_KTh = [R(), R()]
                Vh = sbt(p2, "Vh", [128, 2, 32, 256], BF16); r_Vh = [R(), R()]
                Pt = sbt(p2, "Pt", [128, NS, 512]); r_Pt = [R() for _ in range(NS)]
                Cb = sbt(p2, "Cb", [128, NS, 513]); r_Cb = [R() for _ in range(NS)]
                Wt = sbt(p2, "Wt", [128, NS, 512], BF16)
                r_WtA = [R() for _ in range(NS)]; r_WtB = [R() for _ in range(NS)]
                WTs = sbt(p2, "WTs", [128, NS, 512], BF16); r_WTs = [R() for _ in range(NS)]
                pZ = [pst(p2, f"pZ{i}", [128, 512]) for i in range(2)]; r_pZ = [R(), R()]
                pW = [pst(p2, f"pW{i}", [128, 1024], BF16) for i in range(2)]; r_pW = [R(), R()]
                pOa = [pst(p2, f"pOa{i}", [128, 512]) for i in range(2)]; r_pOa = [R(), R()]
                v_view = v_scr.rearrange("(mb p) w -> p mb w", p=128)
                scale = -1.0 / math.sqrt(128.0)

                def loadKV(h):
                    sl = h % 2
                    dma(KTh[:, sl, :], kT_scr[h], [r_kT_scr[h]], [r_KTh[sl]])
                    for q4 in range(4):
                        dma(Vh[:, sl, q4 * 8:(q4 + 1) * 8, :], v_view[:, q4 * 8:(q4 + 1) * 8, h * 256:(h + 1) * 256],
                            [r_v_scr], [r_Vh[sl]])

                tl = []
                ocnt = 0
                for h in range(8):
                    for s in range(16):
                        j, isB = s // 2, s % 2
                        nkb = 4 * j + 2 + 2 * isB
                        mb = 32 - nkb
                        tiles = []
                        if nkb % 4 == 2:
                            tiles.append((mb, 2))
                            mb += 2
                        while mb < 32:
                            tiles.append((mb, 4))
                            mb += 4
                        for ti, (tmb, nb) in enumerate(tiles):
                            tl.append(dict(h=h, s=s, isB=isB, ti=ti, nt=len(tiles), tmb=tmb, nb=nb, ob=ocnt % 2,
                                           newh=(s == 0 and ti == 0)))
                        ocnt += 1

                def st1(i):
                    t = tl[i]
                    h, s, w, m0_ = t["h"], t["s"], t["nb"] * 128, t["tmb"] * 128
                    ksl = h % 2
                    k = i % NS
                    z = i % 2
                    mm(pZ[z][:, 0:w], QT[:, h, s * 128:(s + 1) * 128], KTh[:, ksl, m0_:m0_ + w], True, True,
                       [r_QT[s], r_KTh[ksl]], [r_pZ[z]])
                    act(Pt[:, k, 0:w], pZ[z][:, 0:w], AF.Sigmoid, [r_pZ[z]], [r_Pt[k]], scale=scale)
                    if t["ti"] == 0:
                        tt("dve", Pt[:, k, 0:256], Pt[:, k, 0:256], msk[:, t["isB"], :], ALU.max,
                           [r_Pt[k], r_msk], [r_Pt[k]])
                        S.op("pool", lambda g: g.memset(Cb[:, k, 0:1], 1.0), writes=[r_Cb[k]])
                    else:
                        pk = (i - 1) % NS
                        pw = tl[i - 1]["nb"] * 128
                        cp("dve", Cb[:, k, 0:1], Cb[:, pk, pw:pw + 1], [r_Cb[pk]], [r_Cb[k]])
                    S.op("dve", lambda g: g.tensor_tensor_scan(
                        out=Cb[:, k, 1:w + 1], data0=Pt[:, k, 0:w], data1=zeros[:, 0:w], initial=Cb[:, k, 0:1],
                        op0=ALU.mult, op1=ALU.add), reads=[r_Pt[k], r_zeros, r_Cb[k]], writes=[r_Cb[k]])
                    tt("pool", Wt[:, k, 0:w], Cb[:, k, 0:w], Cb[:, k, 1:w + 1], ALU.subtract, [r_Cb[k]], [r_WtB[k]])

                def st2(i):
                    t = tl[i]
                    w = t["nb"] * 128
                    k = i % NS
                    z = i % 2
                    for b4 in range(t["nb"]):
                        tr(pW[z][:, b4 * 128:(b4 + 1) * 128], Wt[:, k, b4 * 128:(b4 + 1) * 128],
                           [r_WtB[k]], [r_pW[z]])
                    cp("act", WTs[:, k, 0:w], pW[z][:, 0:w], [r_pW[z]], [r_WTs[k]])

                def st3(i):
                    t = tl[i]
                    h, s, ob = t["h"], t["s"], t["ob"]
                    ksl = h % 2
                    k = i % NS
                    if t["newh"] and h + 1 < 8:
                        loadKV(h + 1)
                    for b4 in range(t["nb"]):
                        first = (t["ti"] == 0 and b4 == 0)
                        last = (t["ti"] == t["nt"] - 1 and b4 == t["nb"] - 1)
                        mm(pOa[ob][:, 0:256], WTs[:, k, b4 * 128:(b4 + 1) * 128], Vh[:, ksl, t["tmb"] + b4, :], first, last,
                           [r_WTs[k], r_Vh[ksl]], [r_pOa[ob]])
                    if t["ti"] == t["nt"] - 1:
                        tt("dve", gog[:, s, h * 256:(h + 1) * 256], pOa[ob][:, 0:256], gog[:, s, h * 256:(h + 1) * 256], ALU.mult,
                           [r_pOa[ob], r_gog[s][h]], [r_gog[s][h]])

                loadKV(0)
                ntl = len(tl)
                for i in range(ntl + 4):
                    if i < ntl:
                        st1(i)
                    if 0 <= i - 3 < ntl:
                        st2(i - 3)
                    if 0 <= i - 4 < ntl:
                        st3(i - 4)
                S.barrier()

            with ExitStack() as p3:
                stg = sbt(p3, "stgE", [128, 4, 512]); r_stg = [R(), R(), R(), R()]
                wo1, r_wo1 = load_w(p3, "wo1", w_out1, 16, D, stg, r_stg, lambda c: None)
                ogT = sbt(p3, "ogTE", [128, 2, 16, 128], BF16); r_ogT = [R(), R()]
                tmp = sbt(p3, "tmpE", [128, D]); r_tmp = R()
                ot = sbt(p3, "otE", [128, 2, D]); r_ot = [R(), R()]
                pY = [pst(p3, f"pYE{i}", [128, 512]) for i in range(2)]; r_pY = [R(), R()]
                pT = [pst(p3, f"pTE{i}", [128, 1024], BF16) for i in range(2)]; r_pT = [R(), R()]

                def e_tr(s):
                    sl = s % 2
                    for hb in range(2):
                        for blk in range(8):
                            wc = hb * 8 + blk
                            tr(pT[hb][:, blk * 128:(blk + 1) * 128], gog[:, s, wc * 128:(wc + 1) * 128],
                               [r_gog[s][wc // 2]], [r_pT[hb]])
                        cp("act" if hb == 0 else "dve", ogT[:, sl, hb * 8:(hb + 1) * 8, :].rearrange("p a b -> p (a b)"),
                           pT[hb][:], [r_pT[hb]], [r_ogT[sl]])

                def e_y(s):
                    sl = s % 2
                    for cg in range(2):
                        for wc in range(16):
                            mm(pY[cg][:], ogT[:, sl, wc, :], wo1[:, wc, cg * 512:(cg + 1) * 512], wc == 0, wc == 15,
                               [r_ogT[sl], r_wo1], [r_pY[cg]])
                    for cg in range(2):
                        act(junk[:, cg * 512:(cg + 1) * 512], pY[cg][:], AF.Square, [r_pY[cg]], [r_junk, r_ss],
                            accum=ss[:, 2 + cg:3 + cg])
                    tt("dve", ss[:, 0:1], ss[:, 2:3], ss[:, 3:4], ALU.add, [r_ss], [r_ss])
                    rstd_from_ss(0, D, r_ss)
                    for cg in range(2):
                        stt(tmp[:, cg * 512:(cg + 1) * 512], pY[cg][:], ss[:, 1:2], gp1[:, cg * 512:(cg + 1) * 512],
                            ALU.mult, ALU.mult, [r_pY[cg], r_ss, r_gp1], [r_tmp])
                    tt("pool", ot[:, sl, :], hb_[:, sl, :], tmp[:], ALU.add, [r_hb[sl], r_tmp], [r_ot[sl]])
                    dma(out[s * 128:(s + 1) * 128, :], ot[:, sl, :], [r_ot[sl]], [r_out])

                load_h1(0, 0)
                load_h1(1, 1)
                e_tr(0)
                for s in range(16):
                    if s + 1 < 16:
                        e_tr(s + 1)
                    e_y(s)
                    if s + 2 < 16:
                        load_h1(s + 2, s % 2)

        with nc.Block() as block:
            S.emit(block, esem, dsem)
    return nc


_NC = None


def _layout_inputs(x, ret_norm_pre, ret_w_in, ret_w_out, ret_norm_post, kv_norm, w_kv,
                   sb_norm_pre, sb_w_in, sb_w_out, sb_norm_post):
    cossin, dtm, qdec, kdec, _ = _host_consts()
    f = lambda a: np.ascontiguousarray(np.asarray(a, dtype=np.float32))
    x = f(x)

    def g8(gv):
        return np.asarray(gv, np.float32).reshape(8, 128).T
    gpre = np.ascontiguousarray(np.stack([g8(ret_norm_pre[0]), g8(kv_norm), g8(sb_norm_pre[0])], axis=1))
    gpost = np.ascontiguousarray(np.stack([np.asarray(ret_norm_post[0], np.float32),
                                           np.asarray(sb_norm_post[0], np.float32)], axis=0))
    common = {
        "w_in0": f(ret_w_in[0]), "w_out0": f(ret_w_out[0]), "w_kv": f(w_kv), "w_in1": f(sb_w_in[0]),
        "w_out1": f(sb_w_out[0]), "gpre": gpre, "gpost": gpost, "cossin": cossin, "dtm": dtm, "qdec": qdec,
        "kdec": kdec,
    }
    maps = []
    for c in range(8):
        b, r = c // 2, c % 2
        m = dict(common)
        m["x"] = x[b]
        m["masks"] = _masks(r)
        m["rid"] = np.array([[r, 1 - r]], np.int32)
        maps.append(m)
    return maps


def _slot_block(s, r):
    j, isB = s // 2, s % 2
    return 4 * j + r if isB == 0 else 4 * j + 3 - r


def kernel(**inputs):
    global _NC
    if _NC is None:
        _NC = build_nc()
    maps = _layout_inputs(**inputs)
    res = run_bass_kernel_spmd(_NC, maps, core_ids=list(range(8)))
    outp = np.empty((4, T, D), np.float32)
    for c in range(8):
        b, r = c // 2, c % 2
        o = res.results[c]["out"]
        for s in range(16):
            qb = _slot_block(s, r)
            outp[b, qb * 128:(qb + 1) * 128, :] = o[s * 128:(s + 1) * 128, :]
    return outp
```
